# Optimizing a Trainium2 kernel written in Bass

```python
import jax, jax.numpy as jnp
from jax import lax
import numpy as np

D_MODEL = 2048
BATCH = 1
SEQ = 8192
DEPTH = 1

N_SUB = 3
D_FF = 5632
POOL_WINDOWS = (2, 4, 8, 16)
POOL_GROUPS = len(POOL_WINDOWS)
POOL_GROUP_W = D_MODEL // 8
POOL_W = POOL_GROUPS * POOL_GROUP_W
HEAD_DIM = 64
N_HEADS = 16
N_KV_HEADS = 2
GQA_GROUP = N_HEADS // N_KV_HEADS
WINDOW = 128
BLK = 128
NUM_BUCKETS = 32
MAX_EXACT = NUM_BUCKETS // 2
REL_MAX_DIST = 128
EPS = 1e-6
NEG_INF = -1e30
IN_SPLITS = (POOL_W, N_HEADS * HEAD_DIM, N_KV_HEADS * HEAD_DIM, N_KV_HEADS * HEAD_DIM, D_MODEL, D_MODEL)
IN_W = sum(IN_SPLITS)

kernel_name = "hybrid_pool_swa_gated_macaron_block"


def rms_norm(x, g):
    xf = x.astype(jnp.float32)
    y = xf * lax.rsqrt(jnp.mean(xf * xf, axis=-1, keepdims=True) + EPS)
    return (y * g.astype(jnp.float32)).astype(x.dtype)


def modulate(h, shift, scale):
    return h * (1 + scale) + shift


def swiglu(h, w_gu, w_down):
    g, u = jnp.split(h @ w_gu, 2, axis=-1)
    return (jax.nn.silu(g) * u) @ w_down


def multiscale_pool(u, pool_mix, pool_scale):
    B, S, _ = u.shape
    uf = u.astype(jnp.float32).reshape(B, S, POOL_GROUPS, POOL_GROUP_W)
    cs = jnp.pad(jnp.cumsum(uf, axis=1), ((0, 0), (1, 0), (0, 0), (0, 0)))
    t1 = np.arange(1, S + 1)
    outs = []
    for gi, w in enumerate(POOL_WINDOWS):
        lo = np.maximum(t1 - w, 0)
        cnt = np.minimum(t1, w).astype(np.float32)[None, :, None]
        win_sum = cs[:, 1:, gi] - cs[:, lo, gi]
        outs.append(win_sum / cnt - uf[:, :, gi])
    pooled = jnp.stack(outs, axis=2).astype(u.dtype)
    mixed = jnp.einsum('bsgc,gcd->bsgd', pooled, pool_mix)
    return mixed.reshape(B, S, POOL_W) * pool_scale


def rel_bucket_band():
    ql = np.arange(BLK)[:, None]
    j = np.arange(2 * BLK)[None, :]
    n = np.clip(BLK + ql - j, 0, None)
    nf = np.maximum(n, 1).astype(np.float32)
    large = MAX_EXACT + (np.log(nf / MAX_EXACT) / np.log(REL_MAX_DIST / MAX_EXACT)
                         * (NUM_BUCKETS - MAX_EXACT)).astype(np.int32)
    large = np.minimum(large, NUM_BUCKETS - 1)
    return np.where(n < MAX_EXACT, n, large).astype(np.int32)


def band_mask(nb):
    qpos = (np.arange(nb)[:, None, None] * BLK + np.arange(BLK)[None, :, None])
    kpos = ((np.arange(nb)[:, None, None] - 1) * BLK + np.arange(2 * BLK)[None, None, :])
    dist = qpos - kpos
    return (dist >= 0) & (dist < WINDOW) & (kpos >= 0)


def swa_sink_attention(q, k, v, q_gain, k_gain, sinks, rel_bias):
    B, S = q.shape[:2]
    nb = S // BLK
    q = rms_norm(q, q_gain)
    k = rms_norm(k, k_gain)
    qb = q.reshape(B, nb, BLK, N_KV_HEADS, GQA_GROUP, HEAD_DIM)

    def band(t):
        tb = t.reshape(B, nb, BLK, N_KV_HEADS, HEAD_DIM)
        prev = jnp.pad(tb, ((0, 0), (1, 0), (0, 0), (0, 0), (0, 0)))[:, :nb]
        return jnp.concatenate([prev, tb], axis=2)

    kband, vband = band(k), band(v)
    logits = jnp.einsum('bnqkgd,bnjkd->bnkgqj', qb, kband).astype(jnp.float32) * (HEAD_DIM ** -0.5)
    bias = rel_bias.astype(jnp.float32)[rel_bucket_band()]
    bias = jnp.transpose(bias, (2, 0, 1)).reshape(N_KV_HEADS, GQA_GROUP, BLK, 2 * BLK)
    logits = logits + bias
    mask = band_mask(nb)[None, :, None, None]
    logits = jnp.where(mask, logits, NEG_INF)
    sink = jnp.broadcast_to(sinks.astype(jnp.float32).reshape(1, 1, N_KV_HEADS, GQA_GROUP, 1, 1),
                            logits.shape[:-1] + (1,))
    p = jax.nn.softmax(jnp.concatenate([logits, sink], axis=-1), axis=-1)[..., :-1]
    out = jnp.einsum('bnkgqj,bnjkd->bnqkgd', p.astype(v.dtype), vband)
    return out.reshape(B, S, N_HEADS * HEAD_DIM)


def setup_inputs(seed: int = 0) -> dict:
    key = jax.random.key(seed)
    ks = jax.random.split(key, 24)
    L = DEPTH

    def dense(k, shape, fan_in):
        return jax.random.normal(k, shape, jnp.float32) * (fan_in ** -0.5)

    def gain(k, shape, s=0.05):
        return 1.0 + s * jax.random.normal(k, shape, jnp.float32)

    return {
        "x": jax.random.normal(ks[0], (BATCH, SEQ, D_MODEL), jnp.float32),
        "c": jax.random.normal(ks[1], (BATCH, D_MODEL), jnp.float32),
        "w_ada": dense(ks[2], (L, D_MODEL, 3 * N_SUB * D_MODEL), D_MODEL),
        "b_ada": 0.02 * jax.random.normal(ks[3], (L, 3 * N_SUB * D_MODEL), jnp.float32),
        "g_ffn1": gain(ks[4], (L, D_MODEL)),
        "w_ffn1_gu": dense(ks[5], (L, D_MODEL, 2 * D_FF), D_MODEL),
        "w_ffn1_down": dense(ks[6], (L, D_FF, D_MODEL), D_FF),
        "g_mix": gain(ks[7], (L, D_MODEL)),
        "w_in": dense(ks[8], (L, D_MODEL, IN_W), D_MODEL),
        "pool_mix": dense(ks[9], (L, POOL_GROUPS, POOL_GROUP_W, POOL_GROUP_W), POOL_GROUP_W),
        "pool_scale": gain(ks[10], (L, POOL_W), 0.1),
        "w_pool_up": dense(ks[11], (L, POOL_W, D_MODEL), POOL_W),
        "q_gain": gain(ks[12], (L, HEAD_DIM)),
        "k_gain": gain(ks[13], (L, HEAD_DIM)),
        "sinks": jax.random.normal(ks[14], (L, N_HEADS), jnp.float32),
        "rel_bias": 0.5 * jax.random.normal(ks[15], (NUM_BUCKETS, N_HEADS), jnp.float32),
        "w_attn_up": dense(ks[16], (L, N_HEADS * HEAD_DIM, D_MODEL), N_HEADS * HEAD_DIM),
        "w_o": dense(ks[17], (L, D_MODEL, D_MODEL), D_MODEL),
        "g_ffn2": gain(ks[18], (L, D_MODEL)),
        "w_ffn2_gu": dense(ks[19], (L, D_MODEL, 2 * D_FF), D_MODEL),
        "w_ffn2_down": dense(ks[20], (L, D_FF, D_MODEL), D_FF),
    }


def reference(x, c, w_ada, b_ada, g_ffn1, w_ffn1_gu, w_ffn1_down, g_mix, w_in, pool_mix,
              pool_scale, w_pool_up, q_gain, k_gain, sinks, rel_bias, w_attn_up, w_o,
              g_ffn2, w_ffn2_gu, w_ffn2_down):
    B, S, D = x.shape
    split_idx = [int(s) for s in np.cumsum(IN_SPLITS)[:-1]]
    for l in range(DEPTH):
        mod = (jax.nn.silu(c) @ w_ada[l] + b_ada[l]).reshape(B, 3 * N_SUB, 1, D)

        h = modulate(rms_norm(x, g_ffn1[l]), mod[:, 0], mod[:, 1])
        x = x + 0.5 * mod[:, 2] * swiglu(h, w_ffn1_gu[l], w_ffn1_down[l])

        h = modulate(rms_norm(x, g_mix[l]), mod[:, 3], mod[:, 4])
        z = h @ w_in[l]
        u_pool, q, k, v, ga, gb = jnp.split(z, split_idx, axis=-1)
        y_pool = multiscale_pool(u_pool, pool_mix[l], pool_scale[l]) @ w_pool_up[l]
        y_attn = swa_sink_attention(
            q.reshape(B, S, N_HEADS, HEAD_DIM),
            k.reshape(B, S, N_KV_HEADS, HEAD_DIM),
            v.reshape(B, S, N_KV_HEADS, HEAD_DIM),
            q_gain[l], k_gain[l], sinks[l], rel_bias) @ w_attn_up[l]
        merged = jax.nn.sigmoid(ga) * y_pool + jax.nn.sigmoid(gb) * y_attn
        x = x + mod[:, 5] * (merged @ w_o[l])

        h = modulate(rms_norm(x, g_ffn2[l]), mod[:, 6], mod[:, 7])
        x = x + 0.5 * mod[:, 8] * swiglu(h, w_ffn2_gu[l], w_ffn2_down[l])
    return x
```

```python
import numpy as np
import concourse.bass as bass
import concourse.mybir as mybir
from concourse.bass_utils import run_bass_kernel_spmd

F32 = mybir.dt.float32
BF16 = mybir.dt.bfloat16
AF = mybir.ActivationFunctionType
ALU = mybir.AluOpType

D = 2048
NCH = 16
DFF = 5632
NJ = DFF // 128
NG = NJ // 2
TT = 1152
TO = 1024
HALO = 128
EPS = 1e-6
TB1 = [(0, 384), (384, 768), (768, 1152)]
TB2 = [(128, 640), (640, 1152)]
SLOT = 4096
NSLOT = 6
NBASE = 4
NCORES = 8
MASKV = -100.0


class Res:
    __slots__ = ("w", "r")

    def __init__(self):
        self.w = None
        self.r = {}


ENGS = ("pe", "act", "dve", "pool", "sp")


class Sched:
    def __init__(self):
        self.plan = True
        self.reset()

    def reset(self):
        self.prog = {e: [] for e in ENGS}
        self.cnt = {e: 0 for e in ENGS}
        self.waited = {e: {} for e in ENGS}
        self.dcnt = {}

    def _wait(self, en, tok):
        if tok is None:
            return
        s, v = tok
        if en == "pe" and s == "pe":
            return
        if self.waited[en].get(s, 0) >= v:
            return
        self.waited[en][s] = v
        self.prog[en].append(("w", s, v))

    def _deps(self, en, reads, writes):
        for r in reads:
            self._wait(en, r.w)
        for w in writes:
            self._wait(en, w.w)
            for s, v in w.r.items():
                self._wait(en, (s, v))

    def _commit(self, tok, reads, writes):
        s, v = tok
        for r in reads:
            if r.r.get(s, 0) < v:
                r.r[s] = v
        for w in writes:
            w.w = tok
            w.r = {}

    def op(self, en, fn, reads=(), writes=()):
        if self.plan:
            return None
        self._deps(en, reads, writes)
        self.cnt[en] += 1
        tok = (en, self.cnt[en])
        self.prog[en].append(("i", fn, en, 1))
        self._commit(tok, reads, writes)
        return tok

    def dma(self, en, fn, sem, reads=(), writes=()):
        if self.plan:
            return None
        self._deps(en, reads, writes)
        self.dcnt[sem] = self.dcnt.get(sem, 0) + 16
        tok = (sem, self.dcnt[sem])
        self.prog[en].append(("i", fn, sem, 16))
        self._commit(tok, reads, writes)
        return tok

    def batch(self, sem, ress):
        if self.plan:
            return
        for r in ress:
            r.w = (sem, self.dcnt[sem])

    def barrier(self, engs=("pe", "act", "dve")):
        if self.plan:
            return
        for a in engs:
            for b in tuple(engs) + ("pool",):
                if a != b and self.cnt[b] > 0:
                    self._wait(a, (b, self.cnt[b]))

    def snapshot(self):
        return dict(self.cnt)

    def stamp_snap(self, res, snap):
        if self.plan:
            return
        for b in ("pe", "act", "dve", "pool"):
            v = snap.get(b, 0)
            if v > 0 and res.r.get(b, 0) < v:
                res.r[b] = v

    def stamp(self, res, engs=("pe", "act", "dve", "pool")):
        if self.plan:
            return
        for b in engs:
            if self.cnt[b] > 0 and res.r.get(b, 0) < self.cnt[b]:
                res.r[b] = self.cnt[b]


def build_program(debug=False):
    nc = bass.Bass("TRN2", target_bir_lowering=False)

    def din(name, shape):
        return nc.dram_tensor(name, list(shape), F32, kind="ExternalInput").ap()

    xT_d = din("xT", (D, TT))
    par_d = din("par", (128, 219))
    cpar_d = din("cpar", (128, 65))
    bias_d = din("bias_tab", (128, 4096))
    sink_d = din("sink_tab", (128, 1024))
    w_ada = din("w_ada", (D, 9 * D))
    w1gu = din("w_ffn1_gu", (D, 2 * DFF))
    w1d = din("w_ffn1_down", (DFF, D))
    w_in = din("w_in", (D, 6400))
    pmix = din("pool_mix", (4, 256, 256))
    wpu = din("w_pool_up", (1024, D))
    wau = din("w_attn_up", (1024, D))
    w_o = din("w_o", (D, D))
    w2gu = din("w_ffn2_gu", (D, 2 * DFF))
    w2d = din("w_ffn2_down", (DFF, D))
    yT_d = nc.dram_tensor("yT", [D, TO], F32, kind="ExternalOutput").ap()
    dbg_d = {}
    if debug:
        for nm, shp in (("d_x1", (D, TT)), ("d_h2", (D, TT)), ("d_attn", (1024, TO)),
                        ("d_mixed", (1024, TO)), ("d_x2", (D, TT)), ("d_mod", (128, 144))):
            dbg_d[nm] = nc.dram_tensor(nm, list(shp), F32, kind="ExternalOutput").ap()

    def kview(w):
        return w.rearrange("(kc p) n -> p kc n", p=128)

    S = Sched()

    NUNITS = 105600
    ctx = []

    import contextlib
    with contextlib.ExitStack() as es:
        arena = es.enter_context(nc.sbuf_tensor("arena", [128, NUNITS], BF16))
        banks = [es.enter_context(nc.psum_tensor(f"ps{i}", [128, 512], F32)) for i in range(8)]
        semnames = list(ENGS[:4]) + [f"slot{i}" for i in range(NSLOT)] + ["ldx", "ldp", "ldt", "lds", "st", "dbg"]
        sems = {n: es.enter_context(nc.semaphore("s_" + n)) for n in semnames}

        off = [0]

        def alloc_bf(n):
            o = off[0]
            off[0] += (n + 15) // 16 * 16
            assert off[0] <= NUNITS, off[0]
            return arena[:, o:o + n]

        def alloc_f32(n):
            o = off[0]
            off[0] += (2 * n + 15) // 16 * 16
            assert off[0] <= NUNITS, off[0]
            return arena[:, o:o + 2 * n].bitcast(F32)

        xT = alloc_f32(NCH * TT).rearrange("p (c t) -> p c t", c=NCH)
        hT = alloc_bf(NCH * TT).rearrange("p (c t) -> p c t", c=NCH)
        ring = [alloc_bf(SLOT) for _ in range(NBASE)]
        par = alloc_f32(219)
        cpar = alloc_f32(65)
        mod = alloc_f32(144)
        der = alloc_f32(16 * 9)
        sc = alloc_bf(16)
        ones = alloc_bf(128)
        bones = alloc_bf(128)
        epsb = alloc_f32(2)
        tmpA = alloc_f32(TT)
        tmpB = alloc_f32(TT)
        tmpC = alloc_f32(TT)
        scratch_base = off[0]
        scratch_end = NUNITS - (NSLOT - NBASE) * SLOT
        for _i in range(NSLOT - NBASE):
            ring.append(arena[:, scratch_end + _i * SLOT: scratch_end + (_i + 1) * SLOT])

        def scratch_alloc(limit=NUNITS):
            o = [scratch_base]
            NUNITS = limit

            def bf(n):
                s = o[0]
                o[0] += (n + 15) // 16 * 16
                assert o[0] <= NUNITS, (o[0], NUNITS)
                return arena[:, s:s + n]

            def f32(n):
                s = o[0]
                o[0] += (2 * n + 15) // 16 * 16
                assert o[0] <= NUNITS, (o[0], NUNITS)
                return arena[:, s:s + 2 * n].bitcast(F32)
            return bf, f32

        c_v = par[:, 0:16]
        bada = par[:, 16:160]
        g_v = [par[:, 160 + 16 * i:176 + 16 * i] for i in range(3)]
        pscale = par[:, 208:216]
        gq = par[:, 216:217]
        gk = par[:, 217:218]
        valid = cpar[:, 0:1]
        invcnt = cpar[:, 1:65].rearrange("p (g t) -> p g t", g=4)
        a_v = [der[:, 0:16], der[:, 32:48], der[:, 64:80]]
        gate_v = [der[:, 16:32], der[:, 48:64], der[:, 80:96]]
        gs_v = [der[:, 96:112], der[:, 112:128], der[:, 128:144]]

        xres = [Res() for _ in range(NCH)]
        hres2 = [[Res() for _ in range(TT // 128)] for _ in range(NCH)]

        def hrc(c, b0, b1):
            return hres2[c][b0 // 128:(b1 + 127) // 128]

        def hr(b0, b1):
            return [r for c in range(NCH) for r in hrc(c, b0, b1)]
        bres = [Res() for _ in range(8)]
        sres = [Res() for _ in range(NSLOT)]
        R = {k: Res() for k in ("par", "cpar", "mod", "der", "sc", "ones", "tmpA", "tmpB", "tmpC")}

        nrm_t = [tmpA[:, 0:512], tmpA[:, 512:1024], tmpB[:, 0:512], tmpB[:, 512:1024]]
        nrm_tr = [Res() for _ in range(4)]
        nrm_r = {b0: Res() for (b0, _b1) in TB1 + TB2}

        psrr = [0]

        def ps_next():
            b = psrr[0]
            psrr[0] = (b + 1) % 7
            return banks[b], bres[b]

        class Ring:
            def __init__(self):
                self.tiles = []
                self.issued = 0
                self.pos = 0
                self.slot_of = {}
                self.open = []
                self.freeq = list(range(NBASE))
                self.freex = list(range(NBASE, NSLOT))
                self.phase = "f1"
                self.last = None

            @staticmethod
            def tile_phase(tag):
                if tag[0] == "ada":
                    return "f1" if tag[1] <= 5 else "mix"
                return "f1" if tag[0] == "f1" else ("f2" if tag[0] == "f2" else "mix")

            def _try_issue(self):
                while self.issued < len(self.tiles):
                    j = self.issued
                    tag, lds = self.tiles[j]
                    ph = self.tile_phase(tag)
                    if self.freeq:
                        sl = self.freeq.pop(0)
                    elif self.freex and ph == self.phase and ph in ("f1", "f2"):
                        sl = self.freex.pop(0)
                    else:
                        return
                    for k, (vf, src) in enumerate(lds):
                        dst = vf(ring[sl])
                        S.dma("pool", (lambda e, dst=dst, src=src: e.dma_start(out=dst, in_=src)),
                              f"slot{sl}", reads=(), writes=[sres[sl]] if k == 0 else [])
                        if k > 0:
                            sres[sl].w = (f"slot{sl}", S.dcnt[f"slot{sl}"])
                    self.slot_of[j] = sl
                    self.issued += 1

            def pop(self, tag, loads):
                if S.plan:
                    self.tiles.append((tag, loads))
                    self.last = None
                    return ring[0], sres[0]
                i = self.pos
                assert self.tiles[i][0] == tag, (self.tiles[i][0], tag)
                self.pos += 1
                self._try_issue()
                assert self.issued > i, ("ring deadlock", tag)
                self.open.append(i)
                self.last = i
                sl = self.slot_of[i]
                return ring[sl], sres[sl]

            def done(self, i):
                if S.plan or i is None:
                    return
                self.open.remove(i)
                sl = self.slot_of[i]
                (self.freeq if sl < NBASE else self.freex).append(sl)
                self._try_issue()

            def retire(self):
                if S.plan:
                    return
                for i in list(self.open):
                    self.done(i)

            def set_phase(self, ph):
                if S.plan:
                    return
                self.phase = ph
                if ph == "f2":
                    for sl in range(NBASE, NSLOT):
                        S.stamp(sres[sl])
                self._try_issue()

        W = Ring()

        def v3(a, b):
            return lambda s: s[:, 0:a * b].rearrange("p (a b) -> p a b", a=a)

        def mm_group(out_ap, pairs, reads, bres_, **kw):
            def fn(e):
                n = len(pairs)
                ins = None
                for i, (l, r) in enumerate(pairs):
                    ins = e.matmul(out_ap, l, r, start=(i == 0), stop=(i == n - 1), **kw)
                return ins
            return S.op("pe", fn, reads=reads, writes=[bres_])

        def mm_group_k(out_ap, pairs, reads_k, common, bres_):
            if S.plan:
                return
            n = len(pairs)

            def unmet(rs, waited):
                out = {}
                for r in rs:
                    t = r.w
                    if t is None or t[0] == "pe":
                        continue
                    if waited.get(t[0], 0) < t[1]:
                        out[t[0]] = max(out.get(t[0], 0), t[1])
                return out
            i = 0
            first = True
            while i < n:
                w = dict(S.waited["pe"])
                need = unmet(list(reads_k[i]) + (list(common) if first else []), w)
                w.update(need)
                j = i + 1
                while j < n and not unmet(reads_k[j], w):
                    j += 1
                seg = list(range(i, j))
                rd = [r for q in seg for r in reads_k[q]] + (list(common) if first else [])

                def fn(e, seg=seg):
                    ins = None
                    for q in seg:
                        l, r = pairs[q]
                        ins = e.matmul(out_ap, l, r, start=(q == 0), stop=(q == n - 1))
                    return ins
                S.op("pe", fn, reads=rd, writes=[bres_])
                first = False
                i = j

        def compute_mod(i, cbs=range(8)):
            for cb in cbs:
                sl, sr = W.pop(("ada", i, cb), [(v3(16, 256), kview(w_ada)[:, :, i * D + cb * 256: i * D + (cb + 1) * 256])])
                t = v3(16, 256)(sl)
                hmod = W.last
                for mm in range(2):
                    ch = cb * 2 + mm
                    mm_group(banks[7][:, ch:ch + 1],
                             [(t[:, kc, mm * 128:(mm + 1) * 128], sc[:, kc:kc + 1]) for kc in range(16)],
                             [sr, R["sc"]], bres[7])
                W.done(hmod)
                if cb == 7:
                    S.op("dve", lambda e, i=i: e.tensor_tensor(out=mod[:, i * 16:(i + 1) * 16], in0=banks[7][:, 0:16],
                                                                in1=bada[:, i * 16:(i + 1) * 16], op=ALU.add),
                         reads=[bres[7], R["par"]], writes=[R["mod"]])
                    finish_mod(i)

        def finish_mod(i):
            if i in (1, 4, 7):
                k = (i - 1) // 3
                S.op("dve", lambda e, i=i, k=k: e.scalar_tensor_tensor(out=a_v[k], in0=mod[:, i * 16:(i + 1) * 16], scalar=1.0,
                                                                        in1=gs_v[k], op0=ALU.add, op1=ALU.mult),
                     reads=[R["mod"], R["der"]], writes=[R["der"]])
            if i in (2, 8):
                k = 0 if i == 2 else 2
                S.op("dve", lambda e, i=i, k=k: e.tensor_scalar(out=gate_v[k], in0=mod[:, i * 16:(i + 1) * 16], scalar1=0.5,
                                                                 scalar2=None, op0=ALU.mult),
                     reads=[R["mod"], R["der"]], writes=[R["der"]])
            if i == 5:
                S.op("dve", lambda e: e.tensor_copy(out=gate_v[1], in_=mod[:, 80:96]),
                     reads=[R["mod"], R["der"]], writes=[R["der"]])

        def rsqrt_dve(out_ap, in_ap, addc, reads, writes):
            S.op("act", lambda e: e.activation(out=out_ap, in_=in_ap, func=AF.Sqrt, bias=epsb[:, 0:1] if addc > 1e-3 else epsb[:, 1:2],
                                               scale=1.0), reads=list(reads) + [R["ones"]], writes=writes)
            S.op("dve", lambda e: e.reciprocal(out=out_ap, in_=out_ap), reads=writes, writes=writes)

        def norm_stats(t0, t1, tbs, act_only=False):
            for c in range(NCH):
                sel = c % 8
                if sel in (0, 3, 6) or act_only:
                    S.op("act", lambda e, c=c: e.activation(out=hT[:, c, t0:t1], in_=xT[:, c, t0:t1], func=AF.Square),
                         reads=[xres[c]], writes=hrc(c, t0, t1))
                elif sel in (1, 4, 7):
                    S.op("dve", lambda e, c=c: e.tensor_tensor(out=hT[:, c, t0:t1], in0=xT[:, c, t0:t1], in1=xT[:, c, t0:t1], op=ALU.mult),
                         reads=[xres[c]], writes=hrc(c, t0, t1))
                else:
                    S.op("pool", lambda e, c=c: e.tensor_tensor(out=hT[:, c, t0:t1], in0=xT[:, c, t0:t1], in1=xT[:, c, t0:t1], op=ALU.mult),
                         reads=[xres[c]], writes=hrc(c, t0, t1))
            for (b0, b1) in tbs:
                mm_group(banks[7][:, 0:b1 - b0], [(ones[:, :], hT[:, c, b0:b1]) for c in range(NCH)],
                         hr(b0, b1) + [R["ones"]], bres[7])
                rsqrt_dve(tmpC[:, b0:b1], banks[7][:, 0:b1 - b0], D * EPS, [bres[7]], [nrm_r[b0]])

        def norm_apply(k, shift_i, tbs, own_stats=False):
            for (b0, b1) in tbs:
                n = b1 - b0
                for c in range(NCH):
                    q = c % 4
                    tq = nrm_t[q][:, 0:n]
                    var = (0, 1, 2, 0, 1, 2, 0, 1, 2, 0, 1, 2, 0, 1, 2, 0)[c]
                    shift_ap = mod[:, shift_i * 16 + c: shift_i * 16 + c + 1]
                    if var == 0:
                        S.op("dve", lambda e, c=c, tq=tq, b0=b0, b1=b1: e.scalar_tensor_tensor(
                            out=tq, in0=xT[:, c, b0:b1], scalar=a_v[k][:, c:c + 1], in1=tmpC[:, b0:b1], op0=ALU.mult, op1=ALU.mult),
                            reads=[xres[c], R["der"]] + ([nrm_r[b0]] if own_stats else list(nrm_r.values())), writes=[nrm_tr[q]])
                        S.op("act", lambda e, c=c, tq=tq, b0=b0, b1=b1, shift_ap=shift_ap: e.activation(
                            out=hT[:, c, b0:b1], in_=tq, func=AF.Identity, bias=shift_ap, scale=1.0),
                            reads=[nrm_tr[q], R["mod"]], writes=hrc(c, b0, b1))
                    else:
                        S.op("pool", lambda e, c=c, tq=tq, b0=b0, b1=b1: e.tensor_tensor(
                            out=tq, in0=xT[:, c, b0:b1], in1=tmpC[:, b0:b1], op=ALU.mult),
                            reads=[xres[c]] + ([nrm_r[b0]] if own_stats else list(nrm_r.values())), writes=[nrm_tr[q]])
                        if var == 1:
                            S.op("act", lambda e, c=c, tq=tq, b0=b0, b1=b1, shift_ap=shift_ap: e.activation(
                                out=hT[:, c, b0:b1], in_=tq, func=AF.Identity, bias=shift_ap, scale=a_v[k][:, c:c + 1]),
                                reads=[nrm_tr[q], R["mod"], R["der"]], writes=hrc(c, b0, b1))
                        else:
                            S.op("dve", lambda e, c=c, tq=tq, b0=b0, b1=b1, shift_ap=shift_ap: e.tensor_scalar(
                                out=hT[:, c, b0:b1], in0=tq, scalar1=a_v[k][:, c:c + 1], scalar2=shift_ap, op0=ALU.mult, op1=ALU.add),
                                reads=[nrm_tr[q], R["der"], R["mod"]], writes=hrc(c, b0, b1))

        def norm_stats_tb(b0, b1):
            for c in range(NCH):
                sel = c % 8
                if sel in (0, 3, 6):
                    S.op("act", lambda e, c=c: e.activation(out=hT[:, c, b0:b1], in_=xT[:, c, b0:b1], func=AF.Square),
                         reads=[xres[c]], writes=hrc(c, b0, b1))
                elif sel in (1, 4, 7):
                    S.op("dve", lambda e, c=c: e.tensor_tensor(out=hT[:, c, b0:b1], in0=xT[:, c, b0:b1], in1=xT[:, c, b0:b1], op=ALU.mult),
                         reads=[xres[c]], writes=hrc(c, b0, b1))
                else:
                    S.op("pool", lambda e, c=c: e.tensor_tensor(out=hT[:, c, b0:b1], in0=xT[:, c, b0:b1], in1=xT[:, c, b0:b1], op=ALU.mult),
                         reads=[xres[c]], writes=hrc(c, b0, b1))
            mm_group(banks[7][:, 0:b1 - b0], [(ones[:, :], hT[:, c, b0:b1]) for c in range(NCH)],
                     hr(b0, b1) + [R["ones"]], bres[7])
            rsqrt_dve(tmpC[:, b0:b1], banks[7][:, 0:b1 - b0], D * EPS, [bres[7]], [nrm_r[b0]])

        def norm_pipe(k, shift_i, tbs):
            for i, (b0, b1) in enumerate(tbs):
                norm_stats_tb(b0, b1)
                if i >= 1:
                    norm_apply(k, shift_i, [tbs[i - 1]], own_stats=True)
            norm_apply(k, shift_i, [tbs[-1]], own_stats=True)

        def norm(k, shift_i, t0, t1, tbs, mid_hook=None):
            norm_stats(t0, t1, tbs)
            norm_apply(k, shift_i, tbs)

        def ffn(name, wgu, wd, gate_k, tbs, hook, final_out=False):
            sbf, sf32 = scratch_alloc(scratch_end)
            act = [[sbf(TT) for _ in range(2)] for _ in range(3)]
            ares = [[Res() for _ in range(2)] for _ in range(3)]
            sg = [sf32(512) for _ in range(2)]
            sgres = [Res() for _ in range(2)]
            sgi = [0]
            xo = [sf32(512) for _ in range(2)]
            xor_ = [Res() for _ in range(2)]

            def make_gu(gi):
                slg, srg = W.pop((name, "g", gi), [(v3(16, 256), kview(wgu)[:, :, gi * 256:(gi + 1) * 256])])
                hg = W.last
                slu, sru = W.pop((name, "u", gi), [(v3(16, 256), kview(wgu)[:, :, DFF + gi * 256: DFF + (gi + 1) * 256])])
                hu = W.last
                tg, tu = v3(16, 256)(slg), v3(16, 256)(slu)
                sl_ = gi % 3

                def unit(jj, b0, b1):
                    n = b1 - b0
                    pg, rg = ps_next()
                    mm_group_k(pg[:, 0:n], [(tg[:, kc, jj * 128:(jj + 1) * 128], hT[:, kc, b0:b1]) for kc in range(16)],
                               [hrc(kc, b0, b1) for kc in range(16)], [srg], rg)
                    pu, ru = ps_next()
                    mm_group_k(pu[:, 0:n], [(tu[:, kc, jj * 128:(jj + 1) * 128], hT[:, kc, b0:b1]) for kc in range(16)],
                               [hrc(kc, b0, b1) for kc in range(16)], [sru], ru)
                    q = sgi[0] % 2
                    sgi[0] += 1
                    S.op("act", lambda e: e.activation(out=sg[q][:, 0:n], in_=pg[:, 0:n], func=AF.Silu),
                         reads=[rg], writes=[sgres[q]])
                    S.op("dve", lambda e: e.tensor_tensor(out=act[sl_][jj][:, b0:b1], in0=sg[q][:, 0:n], in1=pu[:, 0:n], op=ALU.mult),
                         reads=[ru, sgres[q]], writes=[ares[sl_][jj]])
                units = [(lambda jj=jj, b0=b0, b1=b1: unit(jj, b0, b1)) for jj in range(2) for (b0, b1) in tbs]
                return units, (hg, hu)

            def make_down(gi, last):
                sld, srd = W.pop((name, "d", gi), [(v3(2, 2048), wd.rearrange("(j p) n -> p j n", p=128)[:, 2 * gi:2 * gi + 2, :])])
                hd = W.last
                td = v3(2, 2048)(sld)
                sl_ = gi % 3

                def unit(m, ti):
                    b0, b1 = tbs[ti]
                    n = b1 - b0
                    po, ro = ps_next()
                    mm_group(po[:, 0:n], [(td[:, jj, m * 128:(m + 1) * 128], act[sl_][jj][:, b0:b1]) for jj in range(2)],
                             [srd, ares[sl_][0], ares[sl_][1]], ro)
                    if gi >= NG - 2 and m % 3 == 2:
                        qq = (m // 3 + ti) % 2
                        S.op("act", lambda e: e.activation(out=xo[qq][:, 0:n], in_=po[:, 0:n], func=AF.Copy,
                                                           scale=gate_v[gate_k][:, m:m + 1]),
                             reads=[ro, R["der"]], writes=[xor_[qq]])
                        S.op("pool", lambda e: e.tensor_tensor(out=xT[:, m, b0:b1], in0=xT[:, m, b0:b1], in1=xo[qq][:, 0:n], op=ALU.add),
                             reads=[xor_[qq], xres[m]], writes=[xres[m]])
                    else:
                        S.op("dve", lambda e: e.scalar_tensor_tensor(out=xT[:, m, b0:b1], in0=po[:, 0:n], scalar=gate_v[gate_k][:, m:m + 1],
                                                                     in1=xT[:, m, b0:b1], op0=ALU.mult, op1=ALU.add),
                             reads=[ro, xres[m], R["der"]], writes=[xres[m]])
                    if last and final_out and ti == len(tbs) - 1:
                        S.dma("sp", lambda e: e.dma_start(out=yT_d[m * 128:(m + 1) * 128, :], in_=xT[:, m, HALO:TT]),
                              "st", reads=[xres[m]], writes=[])
                units = [(lambda m=m, ti=ti: unit(m, ti)) for m in range(NCH) for ti in range(len(tbs))]
                return units, (hd,)

            dly = 2 if name == "f1" else 1
            for gi in range(NG + dly):
                gus, hgu = make_gu(gi) if gi < NG else ([], ())
                dns, hdn = make_down(gi - dly, gi == NG + dly - 1) if gi >= dly else ([], ())
                ratio = -(-len(dns) // max(1, len(gus)))
                di = 0
                for u_ in gus:
                    u_()
                    for _ in range(ratio):
                        if di < len(dns):
                            dns[di]()
                            di += 1
                while di < len(dns):
                    dns[di]()
                    di += 1
                for h_ in hgu + hdn:
                    W.done(h_)
                hook(gi)

        def dump(name, src_fn, nchunks, res_list):
            if not debug:
                return
            for c in range(nchunks):
                S.dma("sp", lambda e, c=c: e.dma_start(out=dbg_d[name][c * 128:(c + 1) * 128, :], in_=src_fn(c)),
                      "dbg", reads=res_list, writes=[])

        def program():
            psrr[0] = 0
            S.dma("sp", lambda e: e.dma_start(out=par, in_=par_d), "ldp", writes=[R["par"]])
            S.dma("sp", lambda e: e.dma_start(out=cpar, in_=cpar_d), "ldp", writes=[R["cpar"]])
            for c in range(NCH):
                S.dma("sp", lambda e, c=c: e.dma_start(out=xT[:, c, :], in_=xT_d[c * 128:(c + 1) * 128, :]), "ldx",
                      writes=[xres[c]])
            S.batch("ldp", [R["par"], R["cpar"]])
            S.batch("ldx", xres)
            S.op("dve", lambda e: e.memset(ones, 1.0), writes=[R["ones"]])
            S.op("dve", lambda e: e.memset(bones, 0.0), writes=[R["ones"]])
            S.op("dve", lambda e: e.memset(bones[0:64, 0:64], 1.0), writes=[R["ones"]])
            S.op("dve", lambda e: e.memset(bones[64:128, 64:128], 1.0), writes=[R["ones"]])
            S.op("dve", lambda e: e.memset(epsb[:, 0:1], float(D * EPS)), writes=[R["ones"]])
            S.op("dve", lambda e: e.memset(epsb[:, 1:2], float(64 * EPS)), writes=[R["ones"]])
            S.op("act", lambda e: e.activation(out=sc, in_=c_v, func=AF.Silu), reads=[R["par"]], writes=[R["sc"]])
            for k in range(3):
                S.op("dve", lambda e, k=k: e.tensor_scalar(out=gs_v[k], in0=g_v[k], scalar1=float(np.sqrt(D)), scalar2=None,
                                                           op0=ALU.mult), reads=[R["par"]], writes=[R["der"]])
            compute_mod(0)
            norm_stats(0, TT, TB1, act_only=True)
            compute_mod(1)
            norm_apply(0, 0, TB1)

            pend = [(i, cb) for i in (2, 3, 4, 5) for cb in range(8)]

            def hook1(gi):
                n = 4 if gi < 2 else -(-len(pend) // (NG + 2 - gi))
                for _ in range(n):
                    if pend:
                        i, cb = pend.pop(0)
                        compute_mod(i, [cb])

            ffn("f1", w1gu, w1d, 0, TB1, hook1)
            assert not pend
            dump("d_mod", lambda c: mod[:, :], 1, [R["mod"]])
            dump("d_x1", lambda c: xT[:, c, :], NCH, xres)

            snap_f1 = S.snapshot()
            W.set_phase("mix")
            norm_stats(0, TT, TB1)
            norm_apply(1, 3, [(128, 640), (640, 1152), (0, 128)])
            if debug:
                for c in range(NCH):
                    S.op("dve", lambda e, c=c: e.tensor_copy(out=tmpA[:, :], in_=hT[:, c, :]), reads=hrc(c, 0, TT), writes=[R["tmpA"]])
                    S.dma("sp", lambda e, c=c: e.dma_start(out=dbg_d["d_h2"][c * 128:(c + 1) * 128, :], in_=tmpA[:, :]),
                          "dbg", reads=[R["tmpA"]], writes=[])
            mixer(snap_f1)
            S.barrier()
            dump("d_x2", lambda c: xT[:, c, :], NCH, xres)

            W.set_phase("f2")
            norm_stats(HALO, TT, TB2)
            norm_apply(2, 6, TB2)
            ffn("f2", w2gu, w2d, 2, TB2, lambda gi: None, final_out=True)
            if not S.plan:
                S.prog["sp"].append(("w", "st", S.dcnt["st"]))
                if debug:
                    S.prog["sp"].append(("w", "dbg", S.dcnt["dbg"]))

        def mixer(snap):
            def mk():
                r = Res()
                S.stamp_snap(r, snap)
                return r

            def mkc():
                r = Res()
                S.stamp(r)
                return r
            sbf, sf32 = scratch_alloc()
            qa = sbf(8 * TO).rearrange("p (c t) -> p c t", c=8)
            qares = [[mk() for _ in range(8)] for _ in range(8)]
            reg2 = sbf(0)
            bf2, f322 = sbf, sf32
            kT = bf2(2 * TT).rearrange("p (c t) -> p c t", c=2)
            kres = [mk(), mk()]
            vS = bf2(9 * 128).rearrange("p (b f) -> p b f", b=9)
            vres = mk()
            PT = [[[bf2(512) for _ in range(2)] for _ in range(2)] for _ in range(2)]
            ptres = [[[mk() for _ in range(2)] for _ in range(2)] for _ in range(2)]
            ebias = f322(2048)
            ebres = Res()
            esink = f322(1024)
            esres = Res()
            S.stamp(esres)
            S.stamp(ebres)
            e_t = [tmpA[:, 0:512], tmpA[:, 512:1024]]
            e_r = [mkc(), mkc()]
            den_t, rec_t = tmpB[:, 0:512], tmpB[:, 512:1024]
            den_r, rec_r = mkc(), mkc()
            r_t = tmpC[:, 0:512]
            r_r = Res()
            sq_t = tmpC[:, 512:768].bitcast(BF16)
            sq_r = Res()

            S.dma("sp", lambda e: e.dma_start(out=esink, in_=sink_d), "lds", writes=[esres])
            S.op("act", lambda e: e.activation(out=esink, in_=esink, func=AF.Exp), reads=[esres], writes=[esres])

            sq2 = [bf2(512), bf2(512)]
            sq2r = [mk(), mk()]
            r2 = [tmpC[:, 0:512], tmpC[:, 512:1024]]
            r2r = [mkc(), mkc()]
            pend = [None]
            ucount = [0]

            def s1(tile, sr, cc, b0, b1):
                n = b1 - b0
                pq, rq = ps_next()
                mm_group_k(pq[:, 0:n], [(tile[:, kc, cc * 128:(cc + 1) * 128], hT[:, kc, b0:b1]) for kc in range(16)],
                           [hrc(kc, b0, b1) for kc in range(16)], [sr], rq)
                i = ucount[0] % 2
                ucount[0] += 1
                S.op("act", lambda e: e.activation(out=sq2[i][:, 0:n], in_=pq[:, 0:n], func=AF.Square),
                     reads=[rq], writes=[sq2r[i]])
                return (pq, rq, i, n)

            def s2(st, dst, dres, gain):
                pq, rq, i, n = st
                pss, rss = ps_next()
                mm_group(pss[:, 0:n], [(bones[:, :], sq2[i][:, 0:n])], [sq2r[i], R["ones"]], rss)
                rsqrt_dve(r2[i][:, 0:n], pss[:, 0:n], 64 * EPS, [rss], [r2r[i]])
                S.op("dve", lambda e: e.scalar_tensor_tensor(out=dst, in0=pq[:, 0:n], scalar=gain, in1=r2[i][:, 0:n],
                                                             op0=ALU.mult, op1=ALU.mult),
                     reads=[rq, r2r[i], R["par"]], writes=dres)

            def flush():
                if pend[0] is not None:
                    s2(*pend[0])
                    pend[0] = None

            for qp in range(4):
                sl, sr = W.pop(("q", qp), [(v3(16, 256), kview(w_in)[:, :, 1024 + qp * 256: 1024 + (qp + 1) * 256])])
                t = v3(16, 256)(sl)
                for cc in range(2):
                    c = 2 * qp + cc
                    for (b0, b1) in TB2:
                        st = s1(t, sr, cc, b0, b1)
                        flush()
                        pend[0] = (st, qa[:, c, b0 - HALO:b1 - HALO], qares[c][(b0 - HALO) // 128:(b1 - HALO) // 128], gq)
                W.retire()
            kv = kview(w_in)
            loads = []
            for kvh in range(2):
                for h2 in range(2):
                    loads.append(((lambda s, kvh=kvh, h2=h2: v3(16, 256)(s)[:, :, kvh * 128 + h2 * 64: kvh * 128 + h2 * 64 + 64]),
                                  kv[:, :, 2048 + kvh * 64: 2048 + kvh * 64 + 64]))
            sl, sr = W.pop(("k",), loads)
            t = v3(16, 256)(sl)
            for kvh in range(2):
                for (b0, b1) in TB1:
                    st = s1(t, sr, kvh, b0, b1)
                    flush()
                    pend[0] = (st, kT[:, kvh, b0:b1], [kres[kvh]], gk)
            W.retire()
            flush()
            sl, sr = W.pop(("v",), [(v3(16, 128), kv[:, :, 2176:2304])])
            t = v3(16, 128)(sl)
            for b4 in range(3):
                nb = 4 if b4 < 2 else 1
                pv, rv = ps_next()
                for bb in range(nb):
                    t9 = b4 * 4 + bb
                    mm_group(pv[:, bb * 128:(bb + 1) * 128],
                             [(hT[:, kc, t9 * 128:(t9 + 1) * 128], t[:, kc, :]) for kc in range(16)], [sr] + hr(t9 * 128, (t9 + 1) * 128), rv)
                S.op("act", lambda e, pv=pv, b4=b4, nb=nb: e.activation(
                    out=vS[:, b4 * 4:b4 * 4 + nb, :], in_=pv[:, 0:nb * 128].rearrange("p (b f) -> p b f", b=nb), func=AF.Copy),
                    reads=[rv], writes=[vres])
            W.retire()

            def partA(kvh, n, buf):
                q0 = n * 128
                for p in range(2):
                    for kb in range(2):
                        k0 = (n + kb) * 128
                        psS, rS = ps_next()
                        mm_group(psS[:, :].rearrange("p (c q) -> p c q", c=4),
                                 [(kT[p * 64:(p + 1) * 64, kvh, k0:k0 + 128], qa[p * 64:(p + 1) * 64, 4 * kvh:4 * kvh + 4, q0:q0 + 128])],
                                 [kres[kvh]] + [qares[c_][n] for c_ in range(4 * kvh, 4 * kvh + 4)], rS)
                        ei = (p * 2 + kb) % 2
                        S.op("act", lambda e, psS=psS, ei=ei: e.activation(out=e_t[ei], in_=psS[:, :], func=AF.Exp, scale=8.0),
                             reads=[rS], writes=[e_r[ei]])
                        eb = ebias[:, (kb * 2 + p) * 512:(kb * 2 + p + 1) * 512]
                        if n == 0 and kb == 0:
                            S.op("dve", lambda e, ei=ei, eb=eb, p=p, kb=kb: e.scalar_tensor_tensor(
                                out=PT[buf][p][kb], in0=e_t[ei], scalar=valid, in1=eb, op0=ALU.mult, op1=ALU.mult),
                                reads=[e_r[ei], ebres, R["cpar"]], writes=[ptres[buf][p][kb]])
                        else:
                            S.op("pool" if kb == 1 else "dve", lambda e, ei=ei, eb=eb, p=p, kb=kb: e.tensor_tensor(
                                out=PT[buf][p][kb], in0=e_t[ei], in1=eb, op=ALU.mult),
                                reads=[e_r[ei], ebres], writes=[ptres[buf][p][kb]])

            def partB(kvh, n, buf):
                q0 = n * 128
                pO, rO = ps_next()
                pD, rD = ps_next()
                for p in range(2):
                    mm_group(pO[p * 64:(p + 1) * 64, :],
                             [(vS[:, n + kb, kvh * 64:(kvh + 1) * 64], PT[buf][p][kb]) for kb in range(2)],
                             [vres, ptres[buf][p][0], ptres[buf][p][1]], rO, tile_position=(0, p * 64))
                    mm_group(pD[p * 64:(p + 1) * 64, :],
                             [(ones[:, 0:64], PT[buf][p][kb]) for kb in range(2)],
                             [R["ones"], ptres[buf][p][0], ptres[buf][p][1]], rD, tile_position=(0, p * 64))
                S.op("dve", lambda e: e.tensor_tensor(out=den_t, in0=pD[:, :], in1=esink[:, kvh * 512:(kvh + 1) * 512],
                                                       op=ALU.add), reads=[rD, esres], writes=[den_r])
                S.op("dve", lambda e: e.reciprocal(out=rec_t, in_=den_t), reads=[den_r], writes=[rec_r])
                S.op("dve", lambda e: e.tensor_tensor(
                    out=qa[:, 4 * kvh:4 * kvh + 4, q0:q0 + 128], in0=pO[:, :].rearrange("p (c q) -> p c q", c=4),
                    in1=rec_t.rearrange("p (c q) -> p c q", c=4), op=ALU.mult),
                    reads=[rO, rec_r], writes=[qares[c_][n] for c_ in range(4 * kvh, 4 * kvh + 4)])

            it = 0
            for kvh in range(2):
                S.dma("sp", lambda e, kvh=kvh: e.dma_start(out=ebias, in_=bias_d[:, kvh * 2048:(kvh + 1) * 2048]), "ldt",
                      writes=[ebres])
                S.op("act", lambda e: e.activation(out=ebias, in_=ebias, func=AF.Exp), reads=[ebres], writes=[ebres])
                prev = None
                for n in range(8):
                    buf = it % 2
                    it += 1
                    partA(kvh, n, buf)
                    if n % 2 == 0:
                        compute_mod(6, [kvh * 4 + n // 2])
                    if prev is not None:
                        partB(*prev)
                    prev = (kvh, n, buf)
                partB(*prev)
            if debug:
                for c in range(8):
                    S.op("dve", lambda e, c=c: e.tensor_copy(out=tmpA[:, 0:TO], in_=qa[:, c, :]), reads=qares[c] + e_r, writes=[R["tmpA"]] + e_r)
                    S.dma("sp", lambda e, c=c: e.dma_start(out=dbg_d["d_attn"][c * 128:(c + 1) * 128, :], in_=tmpA[:, 0:TO]),
                          "dbg", reads=[R["tmpA"]], writes=[])
            S.barrier()

            sbf2, sf322 = scratch_alloc()
            sbf2(8 * TO)
            mixedT = sbf2(8 * TO).rearrange("p (c t) -> p c t", c=8)
            mres = [Res() for _ in range(8)]
            pooled = [sbf2(TO) for _ in range(2)]
            pres = [Res(), Res()]
            U, T1, T2 = tmpA, tmpB, tmpC
            Ur, T1r, T2r = Res(), Res(), Res()
            for g in range(4):
                w = (2, 4, 8, 16)[g]
                slu_, sru_ = W.pop(("pu_in", g), [(v3(16, 256), kview(w_in)[:, :, g * 256:(g + 1) * 256])])
                t = v3(16, 256)(slu_)
                slm_, srm = W.pop(("pm", g), [(v3(2, 256), pmix[g].rearrange("(k p) n -> p k n", p=128))])
                tpm = v3(2, 256)(slm_)
                for cc in range(2):
                    for (b0, b1) in TB1:
                        n = b1 - b0
                        pu_, ru_ = ps_next()
                        mm_group(pu_[:, 0:n], [(t[:, kc, cc * 128:(cc + 1) * 128], hT[:, kc, b0:b1]) for kc in range(16)],
                                 [sru_] + hr(b0, b1), ru_)
                        S.op("act", lambda e, pu_=pu_, n=n, b0=b0, b1=b1: e.activation(out=U[:, b0:b1], in_=pu_[:, 0:n], func=AF.Copy),
                             reads=[ru_], writes=[Ur])
                    compute_mod(7, [2 * g + cc])
                    S.op("dve", lambda e: e.tensor_scalar(out=U[:, 0:HALO], in0=U[:, 0:HALO], scalar1=valid, scalar2=None, op0=ALU.mult),
                         reads=[Ur, R["cpar"]], writes=[Ur])
                    src, srcr = U, Ur
                    dsts = [(T1, T1r), (T2, T2r)]
                    sh = 1
                    di = 0
                    while sh < w:
                        dst, dstr = dsts[di % 2]
                        di += 1
                        S.op("dve", lambda e, src=src, dst=dst, sh=sh: e.tensor_tensor(out=dst[:, 16:TT], in0=src[:, 16:TT],
                                                                                         in1=src[:, 16 - sh:TT - sh], op=ALU.add),
                             reads=[srcr], writes=[dstr])
                        src, srcr = dst, dstr
                        sh *= 2
                    S.op("dve", lambda e, src=src, cc=cc, w=w: e.scalar_tensor_tensor(out=pooled[cc][:, 16:TO], in0=src[:, HALO + 16:TT],
                                                                                        scalar=1.0 / w, in1=U[:, HALO + 16:TT],
                                                                                        op0=ALU.mult, op1=ALU.subtract),
                         reads=[srcr, Ur], writes=[pres[cc]])
                    dst, dstr = dsts[di % 2]
                    S.op("dve", lambda e, src=src, dst=dst, g=g: e.tensor_tensor(out=dst[:, 0:16], in0=src[:, HALO:HALO + 16],
                                                                                 in1=invcnt[:, g, :], op=ALU.mult),
                         reads=[srcr, R["cpar"]], writes=[dstr])
                    S.op("dve", lambda e, dst=dst, cc=cc: e.tensor_tensor(out=pooled[cc][:, 0:16], in0=dst[:, 0:16],
                                                                          in1=U[:, HALO:HALO + 16], op=ALU.subtract),
                         reads=[dstr, Ur], writes=[pres[cc]])
                for cc2 in range(2):
                    for (b0, b1) in TB2:
                        pm_, rm_ = ps_next()
                        mm_group(pm_[:, :], [(tpm[:, cc, cc2 * 128:(cc2 + 1) * 128], pooled[cc][:, b0 - HALO:b1 - HALO]) for cc in range(2)],
                                 [srm, pres[0], pres[1]], rm_)
                        ch = 2 * g + cc2
                        S.op("act", lambda e, pm_=pm_, ch=ch, b0=b0, b1=b1: e.activation(
                            out=mixedT[:, ch, b0 - HALO:b1 - HALO], in_=pm_[:, :], func=AF.Copy, scale=pscale[:, ch:ch + 1]),
                            reads=[rm_, R["par"]], writes=[mres[ch]])
                W.retire()
            if debug:
                for c in range(8):
                    S.op("dve", lambda e, c=c: e.tensor_copy(out=tmpA[:, 0:TO], in_=mixedT[:, c, :]), reads=[mres[c], Ur], writes=[R["tmpA"], Ur])
                    S.dma("sp", lambda e, c=c: e.dma_start(out=dbg_d["d_mixed"][c * 128:(c + 1) * 128, :], in_=tmpA[:, 0:TO]),
                          "dbg", reads=[R["tmpA"]], writes=[])
            S.barrier()

            merged = [[sbf2(TO) for _ in range(2)] for _ in range(2)]
            mgres = [[Res() for _ in range(2)] for _ in range(2)]
            sga_t = [tmpA[:, 0:512], tmpA[:, 512:1024]]
            sga_r = [Res(), Res()]
            t_t = [tmpB[:, 0:512], tmpB[:, 512:1024]]
            t_r = [Res(), Res()]

            def stage_gu(mp):
                s_, rga = W.pop(("ga", mp), [(v3(16, 256), kview(w_in)[:, :, 2304 + mp * 256: 2304 + (mp + 1) * 256])])
                tga = v3(16, 256)(s_)
                s_, rgb = W.pop(("gb", mp), [(v3(16, 256), kview(w_in)[:, :, 4352 + mp * 256: 4352 + (mp + 1) * 256])])
                tgb = v3(16, 256)(s_)
                s_, rpu = W.pop(("pau", mp), [((lambda s: s[:, 0:2048].rearrange("p (a b) -> p a b", a=8)), kview(wpu)[:, :, mp * 256:(mp + 1) * 256]),
                                              ((lambda s: s[:, 2048:4096].rearrange("p (a b) -> p a b", a=8)), kview(wau)[:, :, mp * 256:(mp + 1) * 256])])
                tpu = s_[:, 0:2048].rearrange("p (a b) -> p a b", a=8)
                tau = s_[:, 2048:4096].rearrange("p (a b) -> p a b", a=8)
                rau = rpu
                sl_ = mp % 2
                for mm in range(2):
                    for (b0, b1) in TB2:
                        o0, o1 = b0 - HALO, b1 - HALO
                        pga, r1 = ps_next()
                        mm_group(pga[:, :], [(tga[:, kc, mm * 128:(mm + 1) * 128], hT[:, kc, b0:b1]) for kc in range(16)], [rga] + hr(b0, b1), r1)
                        pyp, r2 = ps_next()
                        mm_group(pyp[:, :], [(tpu[:, kc, mm * 128:(mm + 1) * 128], mixedT[:, kc, o0:o1]) for kc in range(8)], [rpu] + mres, r2)
                        pgb, r3 = ps_next()
                        mm_group(pgb[:, :], [(tgb[:, kc, mm * 128:(mm + 1) * 128], hT[:, kc, b0:b1]) for kc in range(16)], [rgb] + hr(b0, b1), r3)
                        pya, r4 = ps_next()
                        mm_group(pya[:, :], [(tau[:, kc, mm * 128:(mm + 1) * 128], qa[:, kc, o0:o1]) for kc in range(8)], [rau] + [qares[c_][n_] for c_ in range(8) for n_ in range(8)], r4)
                        S.op("act", lambda e, pga=pga: e.activation(out=sga_t[0], in_=pga[:, :], func=AF.Sigmoid), reads=[r1], writes=[sga_r[0]])
                        S.op("act", lambda e, pgb=pgb: e.activation(out=sga_t[1], in_=pgb[:, :], func=AF.Sigmoid), reads=[r3], writes=[sga_r[1]])
                        S.op("dve", lambda e, pyp=pyp: e.tensor_tensor(out=t_t[0], in0=sga_t[0], in1=pyp[:, :], op=ALU.mult),
                             reads=[sga_r[0], r2], writes=[t_r[0]])
                        S.op("dve", lambda e, pya=pya: e.tensor_tensor(out=t_t[1], in0=sga_t[1], in1=pya[:, :], op=ALU.mult),
                             reads=[sga_r[1], r4], writes=[t_r[1]])
                        S.op("dve", lambda e, sl_=sl_, mm=mm, o0=o0, o1=o1: e.tensor_tensor(out=merged[sl_][mm][:, o0:o1], in0=t_t[0], in1=t_t[1], op=ALU.add),
                             reads=[t_r[0], t_r[1]], writes=[mgres[sl_][mm]])
                W.retire()

            uc = [0]
            xo_t = [tmpC[:, 0:512], tmpC[:, 512:1024]]
            xo_r = [Res(), Res()]

            def stage_down(mp):
                s_, r_ = W.pop(("wo", mp), [(v3(2, 2048), w_o.rearrange("(j p) n -> p j n", p=128)[:, 2 * mp:2 * mp + 2, :])])
                two = v3(2, 2048)(s_)
                sl_ = mp % 2
                for m in range(NCH):
                    for (b0, b1) in TB2:
                        o0, o1 = b0 - HALO, b1 - HALO
                        po, ro = ps_next()
                        mm_group(po[:, :], [(two[:, jj, m * 128:(m + 1) * 128], merged[sl_][jj][:, o0:o1]) for jj in range(2)],
                                 [r_, mgres[sl_][0], mgres[sl_][1]], ro)
                        uc[0] += 1
                        if m % 3 == 2:
                            qq = uc[0] % 2
                            S.op("act", lambda e, po=po, m=m, qq=qq: e.activation(out=xo_t[qq], in_=po[:, :], func=AF.Copy,
                                                                                   scale=gate_v[1][:, m:m + 1]),
                                 reads=[ro, R["der"]], writes=[xo_r[qq]])
                            S.op("pool", lambda e, m=m, b0=b0, b1=b1, qq=qq: e.tensor_tensor(out=xT[:, m, b0:b1], in0=xT[:, m, b0:b1],
                                                                                              in1=xo_t[qq], op=ALU.add),
                                 reads=[xo_r[qq], xres[m]], writes=[xres[m]])
                        else:
                            S.op("dve", lambda e, po=po, m=m, b0=b0, b1=b1: e.scalar_tensor_tensor(
                                out=xT[:, m, b0:b1], in0=po[:, :], scalar=gate_v[1][:, m:m + 1], in1=xT[:, m, b0:b1], op0=ALU.mult, op1=ALU.add),
                                reads=[ro, xres[m], R["der"]], writes=[xres[m]])
                W.retire()

            for mp in range(9):
                if mp < 8:
                    stage_gu(mp)
                    compute_mod(8, [mp])
                if mp >= 1:
                    stage_down(mp - 1)

        S.plan = True
        program()
        S.plan = False
        S.reset()
        program()
        assert W.pos == len(W.tiles), (W.pos, len(W.tiles))

        engmap = {"pe": "tensor", "act": "scalar", "dve": "vector", "pool": "gpsimd", "sp": "sync"}
        with nc.Block() as block:
            def replay(en):
                def body(e):
                    for item in S.prog[en]:
                        if item[0] == "w":
                            e.wait_ge(sems[item[1]], item[2])
                        else:
                            ins = item[1](e)
                            ins.then_inc(sems[item[2]], item[3])
                return body
            block.tensor(replay("pe"))
            block.scalar(replay("act"))
            block.vector(replay("dve"))
            block.gpsimd(replay("pool"))
            block.sync(replay("sp"))
    return nc


def _rel_bucket_band():
    ql = np.arange(128)[:, None]
    j = np.arange(256)[None, :]
    n = np.clip(128 + ql - j, 0, None)
    nf = np.maximum(n, 1).astype(np.float32)
    large = 16 + (np.log(nf / 16) / np.log(128 / 16) * 16).astype(np.int32)
    large = np.minimum(large, 31)
    return np.where(n < 16, n, large).astype(np.int32)


def _host_layout(inputs):
    f = lambda a: np.ascontiguousarray(np.asarray(a, dtype=np.float32))
    x = f(inputs["x"])[0]
    xTfull = np.ascontiguousarray(x.T)
    par = np.zeros((128, 219), np.float32)
    par[:, 0:16] = f(inputs["c"])[0].reshape(16, 128).T
    par[:, 16:160] = f(inputs["b_ada"])[0].reshape(144, 128).T
    for i, nm in enumerate(("g_ffn1", "g_mix", "g_ffn2")):
        par[:, 160 + 16 * i:176 + 16 * i] = f(inputs[nm])[0].reshape(16, 128).T
    par[:, 208:216] = f(inputs["pool_scale"])[0].reshape(8, 128).T
    par[:, 216] = np.tile(f(inputs["q_gain"])[0], 2)
    par[:, 217] = np.tile(f(inputs["k_gain"])[0], 2)
    sinks = f(inputs["sinks"])[0]
    sink_tab = np.zeros((128, 2, 4, 128), np.float32)
    for kvh in range(2):
        for c_ in range(4):
            for p in range(2):
                sink_tab[p * 64:(p + 1) * 64, kvh, c_, :] = sinks[8 * kvh + 2 * c_ + p]
    rb = f(inputs["rel_bias"])
    bucket = _rel_bucket_band()
    ql = np.arange(128)[:, None]
    jb = np.arange(256)[None, :]
    dist = 128 + ql - jb
    ok = (dist >= 0) & (dist < 128)
    bias_tab = np.zeros((128, 2, 2, 2, 4, 128), np.float32)
    for kvh in range(2):
        for kb in range(2):
            for p in range(2):
                for c_ in range(4):
                    h = 8 * kvh + 2 * c_ + p
                    bq = rb[bucket[:, kb * 128:(kb + 1) * 128], h]
                    bq = np.where(ok[:, kb * 128:(kb + 1) * 128], bq, np.float32(MASKV))
                    bias_tab[:, kvh, kb, p, c_, :] = bq.T
    shared = {
        "par": par,
        "bias_tab": np.ascontiguousarray(bias_tab.reshape(128, 4096)),
        "sink_tab": np.ascontiguousarray(sink_tab.reshape(128, 1024)),
        "w_ada": f(inputs["w_ada"])[0],
        "w_ffn1_gu": f(inputs["w_ffn1_gu"])[0],
        "w_ffn1_down": f(inputs["w_ffn1_down"])[0],
        "w_in": f(inputs["w_in"])[0],
        "pool_mix": f(inputs["pool_mix"])[0],
        "w_pool_up": f(inputs["w_pool_up"])[0],
        "w_attn_up": f(inputs["w_attn_up"])[0],
        "w_o": f(inputs["w_o"])[0],
        "w_ffn2_gu": f(inputs["w_ffn2_gu"])[0],
        "w_ffn2_down": f(inputs["w_ffn2_down"])[0],
    }
    in_maps = []
    for core in range(NCORES):
        t0 = core * TO
        xt = np.zeros((D, TT), np.float32)
        if core > 0:
            xt[:, :] = xTfull[:, t0 - HALO:t0 + TO]
        else:
            xt[:, HALO:] = xTfull[:, 0:TO]
        cpar = np.zeros((128, 65), np.float32)
        cpar[:, 0] = 1.0 if core > 0 else 0.0
        for g, w in enumerate((2, 4, 8, 16)):
            for t in range(16):
                cnt = w if core > 0 else min(t + 1, w)
                cpar[:, 1 + g * 16 + t] = np.float32(1.0) / np.float32(cnt)
        m = dict(shared)
        m["xT"] = xt
        m["cpar"] = cpar
        in_maps.append(m)
    return in_maps


_NC_CACHE = {}


def kernel(**inputs):
    in_maps = _host_layout(inputs)
    if "nc" not in _NC_CACHE:
        _NC_CACHE["nc"] = build_program()
    nc = _NC_CACHE["nc"]
    res = run_bass_kernel_spmd(nc, in_maps, core_ids=list(range(NCORES)))
    outs = [np.asarray(r["yT"]) for r in res.results]
    y = np.concatenate([o.T for o in outs], axis=0)
    return np.ascontiguousarray(y[None].astype(np.float32))
```

```python
import numpy as np
import concourse.bass as bass
import concourse.mybir as mybir
from concourse.bass_utils import run_bass_kernel_spmd

F32 = mybir.dt.float32
BF16 = mybir.dt.bfloat16
AF = mybir.ActivationFunctionType
ALU = mybir.AluOpType

D = 2048
NCH = 16
DFF = 5632
NJ = DFF // 128
NG = NJ // 2
TT = 1152
TO = 1024
HALO = 128
EPS = 1e-6
TB1 = [(0, 384), (384, 768), (768, 1152)]
TB2 = [(128, 640), (640, 1152)]
SLOT = 4096
NSLOT = 6
NBASE = 4
NCORES = 8
MASKV = -100.0


class Res:
    __slots__ = ("w", "r")

    def __init__(self):
        self.w = None
        self.r = {}


ENGS = ("pe", "act", "dve", "pool", "sp")


class Sched:
    def __init__(self):
        self.plan = True
        self.reset()

    def reset(self):
        self.prog = {e: [] for e in ENGS}
        self.cnt = {e: 0 for e in ENGS}
        self.waited = {e: {} for e in ENGS}
        self.dcnt = {}

    def _wait(self, en, tok):
        if tok is None:
            return
        s, v = tok
        if en == "pe" and s == "pe":
            return
        if self.waited[en].get(s, 0) >= v:
            return
        self.waited[en][s] = v
        self.prog[en].append(("w", s, v))

    def _deps(self, en, reads, writes):
        for r in reads:
            self._wait(en, r.w)
        for w in writes:
            self._wait(en, w.w)
            for s, v in w.r.items():
                self._wait(en, (s, v))

    def _commit(self, tok, reads, writes):
        s, v = tok
        for r in reads:
            if r.r.get(s, 0) < v:
                r.r[s] = v
        for w in writes:
            w.w = tok
            w.r = {}

    def op(self, en, fn, reads=(), writes=()):
        if self.plan:
            return None
        self._deps(en, reads, writes)
        self.cnt[en] += 1
        tok = (en, self.cnt[en])
        self.prog[en].append(("i", fn, en, 1))
        self._commit(tok, reads, writes)
        return tok

    def dma(self, en, fn, sem, reads=(), writes=()):
        if self.plan:
            return None
        self._deps(en, reads, writes)
        self.dcnt[sem] = self.dcnt.get(sem, 0) + 16
        tok = (sem, self.dcnt[sem])
        self.prog[en].append(("i", fn, sem, 16))
        self._commit(tok, reads, writes)
        return tok

    def batch(self, sem, ress):
        if self.plan:
            return
        for r in ress:
            r.w = (sem, self.dcnt[sem])

    def barrier(self, engs=("pe", "act", "dve")):
        if self.plan:
            return
        for a in engs:
            for b in tuple(engs) + ("pool",):
                if a != b and self.cnt[b] > 0:
                    self._wait(a, (b, self.cnt[b]))

    def snapshot(self):
        return dict(self.cnt)

    def stamp_snap(self, res, snap):
        if self.plan:
            return
        for b in ("pe", "act", "dve", "pool"):
            v = snap.get(b, 0)
            if v > 0 and res.r.get(b, 0) < v:
                res.r[b] = v

    def stamp(self, res, engs=("pe", "act", "dve", "pool")):
        if self.plan:
            return
        for b in engs:
            if self.cnt[b] > 0 and res.r.get(b, 0) < self.cnt[b]:
                res.r[b] = self.cnt[b]


def build_program(debug=False):
    nc = bass.Bass("TRN2", target_bir_lowering=False)

    def din(name, shape):
        return nc.dram_tensor(name, list(shape), F32, kind="ExternalInput").ap()

    xT_d = din("xT", (D, TT))
    par_d = din("par", (128, 219))
    cpar_d = din("cpar", (128, 65))
    bias_d = din("bias_tab", (128, 4096))
    sink_d = din("sink_tab", (128, 1024))
    w_ada = din("w_ada", (D, 9 * D))
    w1gu = din("w_ffn1_gu", (D, 2 * DFF))
    w1d = din("w_ffn1_down", (DFF, D))
    w_in = din("w_in", (D, 6400))
    pmix = din("pool_mix", (4, 256, 256))
    wpu = din("w_pool_up", (1024, D))
    wau = din("w_attn_up", (1024, D))
    w_o = din("w_o", (D, D))
    w2gu = din("w_ffn2_gu", (D, 2 * DFF))
    w2d = din("w_ffn2_down", (DFF, D))
    yT_d = nc.dram_tensor("yT", [D, TO], F32, kind="ExternalOutput").ap()
    dbg_d = {}
    if debug:
        for nm, shp in (("d_x1", (D, TT)), ("d_h2", (D, TT)), ("d_attn", (1024, TO)),
                        ("d_mixed", (1024, TO)), ("d_x2", (D, TT)), ("d_mod", (128, 144))):
            dbg_d[nm] = nc.dram_tensor(nm, list(shp), F32, kind="ExternalOutput").ap()

    def kview(w):
        return w.rearrange("(kc p) n -> p kc n", p=128)

    S = Sched()

    NUNITS = 105600
    ctx = []

    import contextlib
    with contextlib.ExitStack() as es:
        arena = es.enter_context(nc.sbuf_tensor("arena", [128, NUNITS], BF16))
        banks = [es.enter_context(nc.psum_tensor(f"ps{i}", [128, 512], F32)) for i in range(8)]
        semnames = list(ENGS[:4]) + [f"slot{i}" for i in range(NSLOT)] + ["ldx", "ldp", "ldt", "lds", "st", "dbg"]
        sems = {n: es.enter_context(nc.semaphore("s_" + n)) for n in semnames}

        off = [0]

        def alloc_bf(n):
            o = off[0]
            off[0] += (n + 15) // 16 * 16
            assert off[0] <= NUNITS, off[0]
            return arena[:, o:o + n]

        def alloc_f32(n):
            o = off[0]
            off[0] += (2 * n + 15) // 16 * 16
            assert off[0] <= NUNITS, off[0]
            return arena[:, o:o + 2 * n].bitcast(F32)

        xT = alloc_f32(NCH * TT).rearrange("p (c t) -> p c t", c=NCH)
        hT = alloc_bf(NCH * TT).rearrange("p (c t) -> p c t", c=NCH)
        ring = [alloc_bf(SLOT) for _ in range(NBASE)]
        par = alloc_f32(219)
        cpar = alloc_f32(65)
        mod = alloc_f32(144)
        der = alloc_f32(16 * 9)
        sc = alloc_bf(16)
        ones = alloc_bf(128)
        bones = alloc_bf(128)
        epsb = alloc_f32(2)
        tmpA = alloc_f32(TT)
        tmpB = alloc_f32(TT)
        tmpC = alloc_f32(TT)
        scratch_base = off[0]
        scratch_end = NUNITS - (NSLOT - NBASE) * SLOT
        for _i in range(NSLOT - NBASE):
            ring.append(arena[:, scratch_end + _i * SLOT: scratch_end + (_i + 1) * SLOT])

        def scratch_alloc(limit=NUNITS):
            o = [scratch_base]
            NUNITS = limit

            def bf(n):
                s = o[0]
                o[0] += (n + 15) // 16 * 16
                assert o[0] <= NUNITS, (o[0], NUNITS)
                return arena[:, s:s + n]

            def f32(n):
                s = o[0]
                o[0] += (2 * n + 15) // 16 * 16
                assert o[0] <= NUNITS, (o[0], NUNITS)
                return arena[:, s:s + 2 * n].bitcast(F32)
            return bf, f32

        c_v = par[:, 0:16]
        bada = par[:, 16:160]
        g_v = [par[:, 160 + 16 * i:176 + 16 * i] for i in range(3)]
        pscale = par[:, 208:216]
        gq = par[:, 216:217]
        gk = par[:, 217:218]
        valid = cpar[:, 0:1]
        invcnt = cpar[:, 1:65].rearrange("p (g t) -> p g t", g=4)
        a_v = [der[:, 0:16], der[:, 32:48], der[:, 64:80]]
        gate_v = [der[:, 16:32], der[:, 48:64], der[:, 80:96]]
        gs_v = [der[:, 96:112], der[:, 112:128], der[:, 128:144]]

        xres = [Res() for _ in range(NCH)]
        hres2 = [[Res() for _ in range(TT // 128)] for _ in range(NCH)]

        def hrc(c, b0, b1):
            return hres2[c][b0 // 128:(b1 + 127) // 128]

        def hr(b0, b1):
            return [r for c in range(NCH) for r in hrc(c, b0, b1)]
        bres = [Res() for _ in range(8)]
        sres = [Res() for _ in range(NSLOT)]
        R = {k: Res() for k in ("par", "cpar", "mod", "der", "sc", "ones", "tmpA", "tmpB", "tmpC")}

        nrm_t = [tmpA[:, 0:512], tmpA[:, 512:1024], tmpB[:, 0:512], tmpB[:, 512:1024]]
        nrm_tr = [Res() for _ in range(4)]
        nrm_r = {b0: Res() for (b0, _b1) in TB1 + TB2}

        psrr = [0]

        def ps_next():
            b = psrr[0]
            psrr[0] = (b + 1) % 7
            return banks[b], bres[b]

        class Ring:
            def __init__(self):
                self.tiles = []
                self.issued = 0
                self.pos = 0
                self.slot_of = {}
                self.open = []
                self.freeq = list(range(NBASE))
                self.freex = list(range(NBASE, NSLOT))
                self.phase = "f1"
                self.last = None

            @staticmethod
            def tile_phase(tag):
                if tag[0] == "ada":
                    return "f1" if tag[1] <= 5 else "mix"
                return "f1" if tag[0] == "f1" else ("f2" if tag[0] == "f2" else "mix")

            def _try_issue(self):
                while self.issued < len(self.tiles):
                    j = self.issued
                    tag, lds = self.tiles[j]
                    ph = self.tile_phase(tag)
                    if self.freeq:
                        sl = self.freeq.pop(0)
                    elif self.freex and ph == self.phase and ph in ("f1", "f2"):
                        sl = self.freex.pop(0)
                    else:
                        return
                    for k, (vf, src) in enumerate(lds):
                        dst = vf(ring[sl])
                        S.dma("pool", (lambda e, dst=dst, src=src: e.dma_start(out=dst, in_=src)),
                              f"slot{sl}", reads=(), writes=[sres[sl]] if k == 0 else [])
                        if k > 0:
                            sres[sl].w = (f"slot{sl}", S.dcnt[f"slot{sl}"])
                    self.slot_of[j] = sl
                    self.issued += 1

            def pop(self, tag, loads):
                if S.plan:
                    self.tiles.append((tag, loads))
                    self.last = None
                    return ring[0], sres[0]
                i = self.pos
                assert self.tiles[i][0] == tag, (self.tiles[i][0], tag)
                self.pos += 1
                self._try_issue()
                assert self.issued > i, ("ring deadlock", tag)
                self.open.append(i)
                self.last = i
                sl = self.slot_of[i]
                return ring[sl], sres[sl]

            def done(self, i):
                if S.plan or i is None:
                    return
                self.open.remove(i)
                sl = self.slot_of[i]
                (self.freeq if sl < NBASE else self.freex).append(sl)
                self._try_issue()

            def retire(self):
                if S.plan:
                    return
                for i in list(self.open):
                    self.done(i)

            def set_phase(self, ph):
                if S.plan:
                    return
                self.phase = ph
                if ph == "f2":
                    for sl in range(NBASE, NSLOT):
                        S.stamp(sres[sl])
                self._try_issue()

        W = Ring()

        def v3(a, b):
            return lambda s: s[:, 0:a * b].rearrange("p (a b) -> p a b", a=a)

        def mm_group(out_ap, pairs, reads, bres_, **kw):
            def fn(e):
                n = len(pairs)
                ins = None
                for i, (l, r) in enumerate(pairs):
                    ins = e.matmul(out_ap, l, r, start=(i == 0), stop=(i == n - 1), **kw)
                return ins
            return S.op("pe", fn, reads=reads, writes=[bres_])

        def mm_group_k(out_ap, pairs, reads_k, common, bres_):
            if S.plan:
                return
            n = len(pairs)

            def unmet(rs, waited):
                out = {}
                for r in rs:
                    t = r.w
                    if t is None or t[0] == "pe":
                        continue
                    if waited.get(t[0], 0) < t[1]:
                        out[t[0]] = max(out.get(t[0], 0), t[1])
                return out
            i = 0
            first = True
            while i < n:
                w = dict(S.waited["pe"])
                need = unmet(list(reads_k[i]) + (list(common) if first else []), w)
                w.update(need)
                j = i + 1
                while j < n and not unmet(reads_k[j], w):
                    j += 1
                seg = list(range(i, j))
                rd = [r for q in seg for r in reads_k[q]] + (list(common) if first else [])

                def fn(e, seg=seg):
                    ins = None
                    for q in seg:
                        l, r = pairs[q]
                        ins = e.matmul(out_ap, l, r, start=(q == 0), stop=(q == n - 1))
                    return ins
                S.op("pe", fn, reads=rd, writes=[bres_])
                first = False
                i = j

        def compute_mod(i, cbs=range(8)):
            for cb in cbs:
                sl, sr = W.pop(("ada", i, cb), [(v3(16, 256), kview(w_ada)[:, :, i * D + cb * 256: i * D + (cb + 1) * 256])])
                t = v3(16, 256)(sl)
                hmod = W.last
                for mm in range(2):
                    ch = cb * 2 + mm
                    mm_group(banks[7][:, ch:ch + 1],
                             [(t[:, kc, mm * 128:(mm + 1) * 128], sc[:, kc:kc + 1]) for kc in range(16)],
                             [sr, R["sc"]], bres[7])
                W.done(hmod)
                if cb == 7:
                    S.op("dve", lambda e, i=i: e.tensor_tensor(out=mod[:, i * 16:(i + 1) * 16], in0=banks[7][:, 0:16],
                                                                in1=bada[:, i * 16:(i + 1) * 16], op=ALU.add),
                         reads=[bres[7], R["par"]], writes=[R["mod"]])
                    finish_mod(i)

        def finish_mod(i):
            if i in (1, 4, 7):
                k = (i - 1) // 3
                S.op("dve", lambda e, i=i, k=k: e.scalar_tensor_tensor(out=a_v[k], in0=mod[:, i * 16:(i + 1) * 16], scalar=1.0,
                                                                        in1=gs_v[k], op0=ALU.add, op1=ALU.mult),
                     reads=[R["mod"], R["der"]], writes=[R["der"]])
            if i in (2, 8):
                k = 0 if i == 2 else 2
                S.op("dve", lambda e, i=i, k=k: e.tensor_scalar(out=gate_v[k], in0=mod[:, i * 16:(i + 1) * 16], scalar1=0.5,
                                                                 scalar2=None, op0=ALU.mult),
                     reads=[R["mod"], R["der"]], writes=[R["der"]])
            if i == 5:
                S.op("dve", lambda e: e.tensor_copy(out=gate_v[1], in_=mod[:, 80:96]),
                     reads=[R["mod"], R["der"]], writes=[R["der"]])

        def rsqrt_dve(out_ap, in_ap, addc, reads, writes):
            S.op("act", lambda e: e.activation(out=out_ap, in_=in_ap, func=AF.Sqrt, bias=epsb[:, 0:1] if addc > 1e-3 else epsb[:, 1:2],
                                               scale=1.0), reads=list(reads) + [R["ones"]], writes=writes)
            S.op("dve", lambda e: e.reciprocal(out=out_ap, in_=out_ap), reads=writes, writes=writes)

        def norm_stats(t0, t1, tbs, act_only=False):
            for c in range(NCH):
                sel = c % 8
                if sel in (0, 3, 6) or act_only:
                    S.op("act", lambda e, c=c: e.activation(out=hT[:, c, t0:t1], in_=xT[:, c, t0:t1], func=AF.Square),
                         reads=[xres[c]], writes=hrc(c, t0, t1))
                elif sel in (1, 4, 7):
                    S.op("dve", lambda e, c=c: e.tensor_tensor(out=hT[:, c, t0:t1], in0=xT[:, c, t0:t1], in1=xT[:, c, t0:t1], op=ALU.mult),
                         reads=[xres[c]], writes=hrc(c, t0, t1))
                else:
                    S.op("pool", lambda e, c=c: e.tensor_tensor(out=hT[:, c, t0:t1], in0=xT[:, c, t0:t1], in1=xT[:, c, t0:t1], op=ALU.mult),
                         reads=[xres[c]], writes=hrc(c, t0, t1))
            for (b0, b1) in tbs:
                mm_group(banks[7][:, 0:b1 - b0], [(ones[:, :], hT[:, c, b0:b1]) for c in range(NCH)],
                         hr(b0, b1) + [R["ones"]], bres[7])
                rsqrt_dve(tmpC[:, b0:b1], banks[7][:, 0:b1 - b0], D * EPS, [bres[7]], [nrm_r[b0]])

        def norm_apply(k, shift_i, tbs, own_stats=False):
            for (b0, b1) in tbs:
                n = b1 - b0
                for c in range(NCH):
                    q = c % 4
                    tq = nrm_t[q][:, 0:n]
                    var = (0, 1, 2, 0, 1, 2, 0, 1, 2, 0, 1, 2, 0, 1, 2, 0)[c]
                    shift_ap = mod[:, shift_i * 16 + c: shift_i * 16 + c + 1]
                    if var == 0:
                        S.op("dve", lambda e, c=c, tq=tq, b0=b0, b1=b1: e.scalar_tensor_tensor(
                            out=tq, in0=xT[:, c, b0:b1], scalar=a_v[k][:, c:c + 1], in1=tmpC[:, b0:b1], op0=ALU.mult, op1=ALU.mult),
                            reads=[xres[c], R["der"]] + ([nrm_r[b0]] if own_stats else list(nrm_r.values())), writes=[nrm_tr[q]])
                        S.op("act", lambda e, c=c, tq=tq, b0=b0, b1=b1, shift_ap=shift_ap: e.activation(
                            out=hT[:, c, b0:b1], in_=tq, func=AF.Identity, bias=shift_ap, scale=1.0),
                            reads=[nrm_tr[q], R["mod"]], writes=hrc(c, b0, b1))
                    else:
                        S.op("pool", lambda e, c=c, tq=tq, b0=b0, b1=b1: e.tensor_tensor(
                            out=tq, in0=xT[:, c, b0:b1], in1=tmpC[:, b0:b1], op=ALU.mult),
                            reads=[xres[c]] + ([nrm_r[b0]] if own_stats else list(nrm_r.values())), writes=[nrm_tr[q]])
                        if var == 1:
                            S.op("act", lambda e, c=c, tq=tq, b0=b0, b1=b1, shift_ap=shift_ap: e.activation(
                                out=hT[:, c, b0:b1], in_=tq, func=AF.Identity, bias=shift_ap, scale=a_v[k][:, c:c + 1]),
                                reads=[nrm_tr[q], R["mod"], R["der"]], writes=hrc(c, b0, b1))
                        else:
                            S.op("dve", lambda e, c=c, tq=tq, b0=b0, b1=b1, shift_ap=shift_ap: e.tensor_scalar(
                                out=hT[:, c, b0:b1], in0=tq, scalar1=a_v[k][:, c:c + 1], scalar2=shift_ap, op0=ALU.mult, op1=ALU.add),
                                reads=[nrm_tr[q], R["der"], R["mod"]], writes=hrc(c, b0, b1))

        def norm_stats_tb(b0, b1):
            for c in range(NCH):
                sel = c % 8
                if sel in (0, 3, 6):
                    S.op("act", lambda e, c=c: e.activation(out=hT[:, c, b0:b1], in_=xT[:, c, b0:b1], func=AF.Square),
                         reads=[xres[c]], writes=hrc(c, b0, b1))
                elif sel in (1, 4, 7):
                    S.op("dve", lambda e, c=c: e.tensor_tensor(out=hT[:, c, b0:b1], in0=xT[:, c, b0:b1], in1=xT[:, c, b0:b1], op=ALU.mult),
                         reads=[xres[c]], writes=hrc(c, b0, b1))
                else:
                    S.op("pool", lambda e, c=c: e.tensor_tensor(out=hT[:, c, b0:b1], in0=xT[:, c, b0:b1], in1=xT[:, c, b0:b1], op=ALU.mult),
                         reads=[xres[c]], writes=hrc(c, b0, b1))
            mm_group(banks[7][:, 0:b1 - b0], [(ones[:, :], hT[:, c, b0:b1]) for c in range(NCH)],
                     hr(b0, b1) + [R["ones"]], bres[7])
            rsqrt_dve(tmpC[:, b0:b1], banks[7][:, 0:b1 - b0], D * EPS, [bres[7]], [nrm_r[b0]])

        def norm_pipe(k, shift_i, tbs):
            for i, (b0, b1) in enumerate(tbs):
                norm_stats_tb(b0, b1)
                if i >= 1:
                    norm_apply(k, shift_i, [tbs[i - 1]], own_stats=True)
            norm_apply(k, shift_i, [tbs[-1]], own_stats=True)

        def norm(k, shift_i, t0, t1, tbs, mid_hook=None):
            norm_stats(t0, t1, tbs)
            norm_apply(k, shift_i, tbs)

        def ffn(name, wgu, wd, gate_k, tbs, hook, final_out=False):
            sbf, sf32 = scratch_alloc(scratch_end)
            act = [[sbf(TT) for _ in range(2)] for _ in range(3)]
            ares = [[Res() for _ in range(2)] for _ in range(3)]
            sg = [sf32(512) for _ in range(2)]
            sgres = [Res() for _ in range(2)]
            sgi = [0]
            xo = [sf32(512) for _ in range(2)]
            xor_ = [Res() for _ in range(2)]

            def make_gu(gi):
                slg, srg = W.pop((name, "g", gi), [(v3(16, 256), kview(wgu)[:, :, gi * 256:(gi + 1) * 256])])
                hg = W.last
                slu, sru = W.pop((name, "u", gi), [(v3(16, 256), kview(wgu)[:, :, DFF + gi * 256: DFF + (gi + 1) * 256])])
                hu = W.last
                tg, tu = v3(16, 256)(slg), v3(16, 256)(slu)
                sl_ = gi % 3

                def unit(jj, b0, b1):
                    n = b1 - b0
                    pg, rg = ps_next()
                    mm_group_k(pg[:, 0:n], [(tg[:, kc, jj * 128:(jj + 1) * 128], hT[:, kc, b0:b1]) for kc in range(16)],
                               [hrc(kc, b0, b1) for kc in range(16)], [srg], rg)
                    pu, ru = ps_next()
                    mm_group_k(pu[:, 0:n], [(tu[:, kc, jj * 128:(jj + 1) * 128], hT[:, kc, b0:b1]) for kc in range(16)],
                               [hrc(kc, b0, b1) for kc in range(16)], [sru], ru)
                    q = sgi[0] % 2
                    sgi[0] += 1
                    S.op("act", lambda e: e.activation(out=sg[q][:, 0:n], in_=pg[:, 0:n], func=AF.Silu),
                         reads=[rg], writes=[sgres[q]])
                    S.op("dve", lambda e: e.tensor_tensor(out=act[sl_][jj][:, b0:b1], in0=sg[q][:, 0:n], in1=pu[:, 0:n], op=ALU.mult),
                         reads=[ru, sgres[q]], writes=[ares[sl_][jj]])
                units = [(lambda jj=jj, b0=b0, b1=b1: unit(jj, b0, b1)) for jj in range(2) for (b0, b1) in tbs]
                return units, (hg, hu)

            def make_down(gi, last):
                sld, srd = W.pop((name, "d", gi), [(v3(2, 2048), wd.rearrange("(j p) n -> p j n", p=128)[:, 2 * gi:2 * gi + 2, :])])
                hd = W.last
                td = v3(2, 2048)(sld)
                sl_ = gi % 3

                def unit(m, ti):
                    b0, b1 = tbs[ti]
                    n = b1 - b0
                    po, ro = ps_next()
                    mm_group(po[:, 0:n], [(td[:, jj, m * 128:(m + 1) * 128], act[sl_][jj][:, b0:b1]) for jj in range(2)],
                             [srd, ares[sl_][0], ares[sl_][1]], ro)
                    if gi >= NG - 2 and m % 3 == 2:
                        qq = (m // 3 + ti) % 2
                        S.op("act", lambda e: e.activation(out=xo[qq][:, 0:n], in_=po[:, 0:n], func=AF.Copy,
                                                           scale=gate_v[gate_k][:, m:m + 1]),
                             reads=[ro, R["der"]], writes=[xor_[qq]])
                        S.op("pool", lambda e: e.tensor_tensor(out=xT[:, m, b0:b1], in0=xT[:, m, b0:b1], in1=xo[qq][:, 0:n], op=ALU.add),
                             reads=[xor_[qq], xres[m]], writes=[xres[m]])
                    else:
                        S.op("dve", lambda e: e.scalar_tensor_tensor(out=xT[:, m, b0:b1], in0=po[:, 0:n], scalar=gate_v[gate_k][:, m:m + 1],
                                                                     in1=xT[:, m, b0:b1], op0=ALU.mult, op1=ALU.add),
                             reads=[ro, xres[m], R["der"]], writes=[xres[m]])
                    if last and final_out and ti == len(tbs) - 1:
                        S.dma("sp", lambda e: e.dma_start(out=yT_d[m * 128:(m + 1) * 128, :], in_=xT[:, m, HALO:TT]),
                              "st", reads=[xres[m]], writes=[])
                units = [(lambda m=m, ti=ti: unit(m, ti)) for m in range(NCH) for ti in range(len(tbs))]
                return units, (hd,)

            sched = {}
            for gi in range(NG + 1):
                if name == "f1":
                    sched[gi] = [] if gi < 2 else ([gi - 2] if gi < 6 else ([4, 5] if gi == 6 else [gi - 1]))
                else:
                    sched[gi] = [gi - 1] if gi >= 1 else []
            for gi in range(NG + 1):
                gus, hgu = make_gu(gi) if gi < NG else ([], ())
                dns, hdn = [], ()
                for dg in sched[gi]:
                    d_, h_ = make_down(dg, dg == NG - 1)
                    dns = dns + d_
                    hdn = hdn + h_
                ratio = -(-len(dns) // max(1, len(gus)))
                di = 0
                for u_ in gus:
                    u_()
                    for _ in range(ratio):
                        if di < len(dns):
                            dns[di]()
                            di += 1
                while di < len(dns):
                    dns[di]()
                    di += 1
                for h_ in hgu + hdn:
                    W.done(h_)
                hook(gi)
            if name == "f1":
                hook(NG + 1)

        def dump(name, src_fn, nchunks, res_list):
            if not debug:
                return
            for c in range(nchunks):
                S.dma("sp", lambda e, c=c: e.dma_start(out=dbg_d[name][c * 128:(c + 1) * 128, :], in_=src_fn(c)),
                      "dbg", reads=res_list, writes=[])

        def program():
            psrr[0] = 0
            S.dma("sp", lambda e: e.dma_start(out=par, in_=par_d), "ldp", writes=[R["par"]])
            S.dma("sp", lambda e: e.dma_start(out=cpar, in_=cpar_d), "ldp", writes=[R["cpar"]])
            for c in range(NCH):
                S.dma("sp", lambda e, c=c: e.dma_start(out=xT[:, c, :], in_=xT_d[c * 128:(c + 1) * 128, :]), "ldx",
                      writes=[xres[c]])
            S.batch("ldp", [R["par"], R["cpar"]])
            S.batch("ldx", xres)
            S.op("dve", lambda e: e.memset(ones, 1.0), writes=[R["ones"]])
            S.op("dve", lambda e: e.memset(bones, 0.0), writes=[R["ones"]])
            S.op("dve", lambda e: e.memset(bones[0:64, 0:64], 1.0), writes=[R["ones"]])
            S.op("dve", lambda e: e.memset(bones[64:128, 64:128], 1.0), writes=[R["ones"]])
            S.op("dve", lambda e: e.memset(epsb[:, 0:1], float(D * EPS)), writes=[R["ones"]])
            S.op("dve", lambda e: e.memset(epsb[:, 1:2], float(64 * EPS)), writes=[R["ones"]])
            S.op("act", lambda e: e.activation(out=sc, in_=c_v, func=AF.Silu), reads=[R["par"]], writes=[R["sc"]])
            for k in range(3):
                S.op("dve", lambda e, k=k: e.tensor_scalar(out=gs_v[k], in0=g_v[k], scalar1=float(np.sqrt(D)), scalar2=None,
                                                           op0=ALU.mult), reads=[R["par"]], writes=[R["der"]])
            compute_mod(0)
            norm_stats(0, TT, TB1, act_only=True)
            compute_mod(1)
            norm_apply(0, 0, TB1)

            pend = [(i, cb) for i in (2, 3, 4, 5) for cb in range(8)]

            def hook1(gi):
                n = 4 if gi < 2 else -(-len(pend) // (NG + 2 - gi))
                for _ in range(n):
                    if pend:
                        i, cb = pend.pop(0)
                        compute_mod(i, [cb])

            ffn("f1", w1gu, w1d, 0, TB1, hook1)
            assert not pend
            dump("d_mod", lambda c: mod[:, :], 1, [R["mod"]])
            dump("d_x1", lambda c: xT[:, c, :], NCH, xres)

            snap_f1 = S.snapshot()
            W.set_phase("mix")
            norm_stats(0, TT, TB1)
            norm_apply(1, 3, [(128, 640), (640, 1152), (0, 128)])
            if debug:
                for c in range(NCH):
                    S.op("dve", lambda e, c=c: e.tensor_copy(out=tmpA[:, :], in_=hT[:, c, :]), reads=hrc(c, 0, TT), writes=[R["tmpA"]])
                    S.dma("sp", lambda e, c=c: e.dma_start(out=dbg_d["d_h2"][c * 128:(c + 1) * 128, :], in_=tmpA[:, :]),
                          "dbg", reads=[R["tmpA"]], writes=[])
            mixer(snap_f1)
            S.barrier()
            dump("d_x2", lambda c: xT[:, c, :], NCH, xres)

            W.set_phase("f2")
            norm_stats(HALO, TT, TB2)
            norm_apply(2, 6, TB2)
            ffn("f2", w2gu, w2d, 2, TB2, lambda gi: None, final_out=True)
            if not S.plan:
                S.prog["sp"].append(("w", "st", S.dcnt["st"]))
                if debug:
                    S.prog["sp"].append(("w", "dbg", S.dcnt["dbg"]))

        def mixer(snap):
            def mk():
                r = Res()
                S.stamp_snap(r, snap)
                return r

            def mkc():
                r = Res()
                S.stamp(r)
                return r
            sbf, sf32 = scratch_alloc()
            qa = sbf(8 * TO).rearrange("p (c t) -> p c t", c=8)
            qares = [[mk() for _ in range(8)] for _ in range(8)]
            reg2 = sbf(0)
            bf2, f322 = sbf, sf32
            kT = bf2(2 * TT).rearrange("p (c t) -> p c t", c=2)
            kres = [mk(), mk()]
            vS = bf2(9 * 128).rearrange("p (b f) -> p b f", b=9)
            vres = mk()
            PT = [[[bf2(512) for _ in range(2)] for _ in range(2)] for _ in range(2)]
            ptres = [[[mk() for _ in range(2)] for _ in range(2)] for _ in range(2)]
            ebias = f322(2048)
            ebres = Res()
            esink = f322(1024)
            esres = Res()
            S.stamp(esres)
            S.stamp(ebres)
            e_t = [tmpA[:, 0:512], tmpA[:, 512:1024]]
            e_r = [mkc(), mkc()]
            den_t, rec_t = tmpB[:, 0:512], tmpB[:, 512:1024]
            den_r, rec_r = mkc(), mkc()
            r_t = tmpC[:, 0:512]
            r_r = Res()
            sq_t = tmpC[:, 512:768].bitcast(BF16)
            sq_r = Res()

            S.dma("sp", lambda e: e.dma_start(out=esink, in_=sink_d), "lds", writes=[esres])
            S.op("act", lambda e: e.activation(out=esink, in_=esink, func=AF.Exp), reads=[esres], writes=[esres])

            sq2 = [bf2(512), bf2(512)]
            sq2r = [mk(), mk()]
            r2 = [tmpC[:, 0:512], tmpC[:, 512:1024]]
            r2r = [mkc(), mkc()]
            pend = [None]
            ucount = [0]

            def s1(tile, sr, cc, b0, b1):
                n = b1 - b0
                pq, rq = ps_next()
                mm_group_k(pq[:, 0:n], [(tile[:, kc, cc * 128:(cc + 1) * 128], hT[:, kc, b0:b1]) for kc in range(16)],
                           [hrc(kc, b0, b1) for kc in range(16)], [sr], rq)
                i = ucount[0] % 2
                ucount[0] += 1
                S.op("act", lambda e: e.activation(out=sq2[i][:, 0:n], in_=pq[:, 0:n], func=AF.Square),
                     reads=[rq], writes=[sq2r[i]])
                return (pq, rq, i, n)

            def s2(st, dst, dres, gain):
                pq, rq, i, n = st
                pss, rss = ps_next()
                mm_group(pss[:, 0:n], [(bones[:, :], sq2[i][:, 0:n])], [sq2r[i], R["ones"]], rss)
                rsqrt_dve(r2[i][:, 0:n], pss[:, 0:n], 64 * EPS, [rss], [r2r[i]])
                S.op("dve", lambda e: e.scalar_tensor_tensor(out=dst, in0=pq[:, 0:n], scalar=gain, in1=r2[i][:, 0:n],
                                                             op0=ALU.mult, op1=ALU.mult),
                     reads=[rq, r2r[i], R["par"]], writes=dres)

            def flush():
                if pend[0] is not None:
                    s2(*pend[0])
                    pend[0] = None

            for qp in range(4):
                sl, sr = W.pop(("q", qp), [(v3(16, 256), kview(w_in)[:, :, 1024 + qp * 256: 1024 + (qp + 1) * 256])])
                t = v3(16, 256)(sl)
                for cc in range(2):
                    c = 2 * qp + cc
                    for (b0, b1) in TB2:
                        st = s1(t, sr, cc, b0, b1)
                        flush()
                        pend[0] = (st, qa[:, c, b0 - HALO:b1 - HALO], qares[c][(b0 - HALO) // 128:(b1 - HALO) // 128], gq)
                W.retire()
            kv = kview(w_in)
            loads = []
            for kvh in range(2):
                for h2 in range(2):
                    loads.append(((lambda s, kvh=kvh, h2=h2: v3(16, 256)(s)[:, :, kvh * 128 + h2 * 64: kvh * 128 + h2 * 64 + 64]),
                                  kv[:, :, 2048 + kvh * 64: 2048 + kvh * 64 + 64]))
            sl, sr = W.pop(("k",), loads)
            t = v3(16, 256)(sl)
            for kvh in range(2):
                for (b0, b1) in TB1:
                    st = s1(t, sr, kvh, b0, b1)
                    flush()
                    pend[0] = (st, kT[:, kvh, b0:b1], [kres[kvh]], gk)
            W.retire()
            flush()
            sl, sr = W.pop(("v",), [(v3(16, 128), kv[:, :, 2176:2304])])
            t = v3(16, 128)(sl)
            for b4 in range(3):
                nb = 4 if b4 < 2 else 1
                pv, rv = ps_next()
                for bb in range(nb):
                    t9 = b4 * 4 + bb
                    mm_group(pv[:, bb * 128:(bb + 1) * 128],
                             [(hT[:, kc, t9 * 128:(t9 + 1) * 128], t[:, kc, :]) for kc in range(16)], [sr] + hr(t9 * 128, (t9 + 1) * 128), rv)
                S.op("act", lambda e, pv=pv, b4=b4, nb=nb: e.activation(
                    out=vS[:, b4 * 4:b4 * 4 + nb, :], in_=pv[:, 0:nb * 128].rearrange("p (b f) -> p b f", b=nb), func=AF.Copy),
                    reads=[rv], writes=[vres])
            W.retire()

            def partA(kvh, n, buf):
                q0 = n * 128
                for p in range(2):
                    for kb in range(2):
                        k0 = (n + kb) * 128
                        psS, rS = ps_next()
                        mm_group(psS[:, :].rearrange("p (c q) -> p c q", c=4),
                                 [(kT[p * 64:(p + 1) * 64, kvh, k0:k0 + 128], qa[p * 64:(p + 1) * 64, 4 * kvh:4 * kvh + 4, q0:q0 + 128])],
                                 [kres[kvh]] + [qares[c_][n] for c_ in range(4 * kvh, 4 * kvh + 4)], rS)
                        ei = (p * 2 + kb) % 2
                        S.op("act", lambda e, psS=psS, ei=ei: e.activation(out=e_t[ei], in_=psS[:, :], func=AF.Exp, scale=8.0),
                             reads=[rS], writes=[e_r[ei]])
                        eb = ebias[:, (kb * 2 + p) * 512:(kb * 2 + p + 1) * 512]
                        if n == 0 and kb == 0:
                            S.op("dve", lambda e, ei=ei, eb=eb, p=p, kb=kb: e.scalar_tensor_tensor(
                                out=PT[buf][p][kb], in0=e_t[ei], scalar=valid, in1=eb, op0=ALU.mult, op1=ALU.mult),
                                reads=[e_r[ei], ebres, R["cpar"]], writes=[ptres[buf][p][kb]])
                        else:
                            S.op("pool" if kb == 1 else "dve", lambda e, ei=ei, eb=eb, p=p, kb=kb: e.tensor_tensor(
                                out=PT[buf][p][kb], in0=e_t[ei], in1=eb, op=ALU.mult),
                                reads=[e_r[ei], ebres], writes=[ptres[buf][p][kb]])

            def partB(kvh, n, buf):
                q0 = n * 128
                pO, rO = ps_next()
                pD, rD = ps_next()
                for p in range(2):
                    mm_group(pO[p * 64:(p + 1) * 64, :],
                             [(vS[:, n + kb, kvh * 64:(kvh + 1) * 64], PT[buf][p][kb]) for kb in range(2)],
                             [vres, ptres[buf][p][0], ptres[buf][p][1]], rO, tile_position=(0, p * 64))
                    mm_group(pD[p * 64:(p + 1) * 64, :],
                             [(ones[:, 0:64], PT[buf][p][kb]) for kb in range(2)],
                             [R["ones"], ptres[buf][p][0], ptres[buf][p][1]], rD, tile_position=(0, p * 64))
                S.op("dve", lambda e: e.tensor_tensor(out=den_t, in0=pD[:, :], in1=esink[:, kvh * 512:(kvh + 1) * 512],
                                                       op=ALU.add), reads=[rD, esres], writes=[den_r])
                S.op("dve", lambda e: e.reciprocal(out=rec_t, in_=den_t), reads=[den_r], writes=[rec_r])
                S.op("dve", lambda e: e.tensor_tensor(
                    out=qa[:, 4 * kvh:4 * kvh + 4, q0:q0 + 128], in0=pO[:, :].rearrange("p (c q) -> p c q", c=4),
                    in1=rec_t.rearrange("p (c q) -> p c q", c=4), op=ALU.mult),
                    reads=[rO, rec_r], writes=[qares[c_][n] for c_ in range(4 * kvh, 4 * kvh + 4)])

            it = 0
            for kvh in range(2):
                S.dma("sp", lambda e, kvh=kvh: e.dma_start(out=ebias, in_=bias_d[:, kvh * 2048:(kvh + 1) * 2048]), "ldt",
                      writes=[ebres])
                S.op("act", lambda e: e.activation(out=ebias, in_=ebias, func=AF.Exp), reads=[ebres], writes=[ebres])
                prev = None
                for n in range(8):
                    buf = it % 2
                    it += 1
                    partA(kvh, n, buf)
                    if n % 2 == 0:
                        compute_mod(6, [kvh * 4 + n // 2])
                    if prev is not None:
                        partB(*prev)
                    prev = (kvh, n, buf)
                partB(*prev)
            if debug:
                for c in range(8):
                    S.op("dve", lambda e, c=c: e.tensor_copy(out=tmpA[:, 0:TO], in_=qa[:, c, :]), reads=qares[c] + e_r, writes=[R["tmpA"]] + e_r)
                    S.dma("sp", lambda e, c=c: e.dma_start(out=dbg_d["d_attn"][c * 128:(c + 1) * 128, :], in_=tmpA[:, 0:TO]),
                          "dbg", reads=[R["tmpA"]], writes=[])
            S.barrier()

            sbf2, sf322 = scratch_alloc()
            sbf2(8 * TO)
            mixedT = sbf2(8 * TO).rearrange("p (c t) -> p c t", c=8)
            mres = [Res() for _ in range(8)]
            pooled = [sbf2(TO) for _ in range(2)]
            pres = [Res(), Res()]
            U, T1, T2 = tmpA, tmpB, tmpC
            Ur, T1r, T2r = Res(), Res(), Res()
            for g in range(4):
                w = (2, 4, 8, 16)[g]
                slu_, sru_ = W.pop(("pu_in", g), [(v3(16, 256), kview(w_in)[:, :, g * 256:(g + 1) * 256])])
                t = v3(16, 256)(slu_)
                slm_, srm = W.pop(("pm", g), [(v3(2, 256), pmix[g].rearrange("(k p) n -> p k n", p=128))])
                tpm = v3(2, 256)(slm_)
                for cc in range(2):
                    for (b0, b1) in TB1:
                        n = b1 - b0
                        pu_, ru_ = ps_next()
                        mm_group(pu_[:, 0:n], [(t[:, kc, cc * 128:(cc + 1) * 128], hT[:, kc, b0:b1]) for kc in range(16)],
                                 [sru_] + hr(b0, b1), ru_)
                        S.op("act", lambda e, pu_=pu_, n=n, b0=b0, b1=b1: e.activation(out=U[:, b0:b1], in_=pu_[:, 0:n], func=AF.Copy),
                             reads=[ru_], writes=[Ur])
                    compute_mod(7, [2 * g + cc])
                    S.op("dve", lambda e: e.tensor_scalar(out=U[:, 0:HALO], in0=U[:, 0:HALO], scalar1=valid, scalar2=None, op0=ALU.mult),
                         reads=[Ur, R["cpar"]], writes=[Ur])
                    src, srcr = U, Ur
                    dsts = [(T1, T1r), (T2, T2r)]
                    sh = 1
                    di = 0
                    while sh < w:
                        dst, dstr = dsts[di % 2]
                        di += 1
                        S.op("dve", lambda e, src=src, dst=dst, sh=sh: e.tensor_tensor(out=dst[:, 16:TT], in0=src[:, 16:TT],
                                                                                         in1=src[:, 16 - sh:TT - sh], op=ALU.add),
                             reads=[srcr], writes=[dstr])
                        src, srcr = dst, dstr
                        sh *= 2
                    S.op("dve", lambda e, src=src, cc=cc, w=w: e.scalar_tensor_tensor(out=pooled[cc][:, 16:TO], in0=src[:, HALO + 16:TT],
                                                                                        scalar=1.0 / w, in1=U[:, HALO + 16:TT],
                                                                                        op0=ALU.mult, op1=ALU.subtract),
                         reads=[srcr, Ur], writes=[pres[cc]])
                    dst, dstr = dsts[di % 2]
                    S.op("dve", lambda e, src=src, dst=dst, g=g: e.tensor_tensor(out=dst[:, 0:16], in0=src[:, HALO:HALO + 16],
                                                                                 in1=invcnt[:, g, :], op=ALU.mult),
                         reads=[srcr, R["cpar"]], writes=[dstr])
                    S.op("dve", lambda e, dst=dst, cc=cc: e.tensor_tensor(out=pooled[cc][:, 0:16], in0=dst[:, 0:16],
                                                                          in1=U[:, HALO:HALO + 16], op=ALU.subtract),
                         reads=[dstr, Ur], writes=[pres[cc]])
                for cc2 in range(2):
                    for (b0, b1) in TB2:
                        pm_, rm_ = ps_next()
                        mm_group(pm_[:, :], [(tpm[:, cc, cc2 * 128:(cc2 + 1) * 128], pooled[cc][:, b0 - HALO:b1 - HALO]) for cc in range(2)],
                                 [srm, pres[0], pres[1]], rm_)
                        ch = 2 * g + cc2
                        S.op("act", lambda e, pm_=pm_, ch=ch, b0=b0, b1=b1: e.activation(
                            out=mixedT[:, ch, b0 - HALO:b1 - HALO], in_=pm_[:, :], func=AF.Copy, scale=pscale[:, ch:ch + 1]),
                            reads=[rm_, R["par"]], writes=[mres[ch]])
                W.retire()
            if debug:
                for c in range(8):
                    S.op("dve", lambda e, c=c: e.tensor_copy(out=tmpA[:, 0:TO], in_=mixedT[:, c, :]), reads=[mres[c], Ur], writes=[R["tmpA"], Ur])
                    S.dma("sp", lambda e, c=c: e.dma_start(out=dbg_d["d_mixed"][c * 128:(c + 1) * 128, :], in_=tmpA[:, 0:TO]),
                          "dbg", reads=[R["tmpA"]], writes=[])
            S.barrier()

            merged = [[sbf2(TO) for _ in range(2)] for _ in range(2)]
            mgres = [[Res() for _ in range(2)] for _ in range(2)]
            sga_t = [tmpA[:, 0:512], tmpA[:, 512:1024]]
            sga_r = [Res(), Res()]
            t_t = [tmpB[:, 0:512], tmpB[:, 512:1024]]
            t_r = [Res(), Res()]

            def stage_gu(mp):
                s_, rga = W.pop(("ga", mp), [(v3(16, 256), kview(w_in)[:, :, 2304 + mp * 256: 2304 + (mp + 1) * 256])])
                tga = v3(16, 256)(s_)
                s_, rgb = W.pop(("gb", mp), [(v3(16, 256), kview(w_in)[:, :, 4352 + mp * 256: 4352 + (mp + 1) * 256])])
                tgb = v3(16, 256)(s_)
                s_, rpu = W.pop(("pau", mp), [((lambda s: s[:, 0:2048].rearrange("p (a b) -> p a b", a=8)), kview(wpu)[:, :, mp * 256:(mp + 1) * 256]),
                                              ((lambda s: s[:, 2048:4096].rearrange("p (a b) -> p a b", a=8)), kview(wau)[:, :, mp * 256:(mp + 1) * 256])])
                tpu = s_[:, 0:2048].rearrange("p (a b) -> p a b", a=8)
                tau = s_[:, 2048:4096].rearrange("p (a b) -> p a b", a=8)
                rau = rpu
                sl_ = mp % 2
                for mm in range(2):
                    for (b0, b1) in TB2:
                        o0, o1 = b0 - HALO, b1 - HALO
                        pga, r1 = ps_next()
                        mm_group(pga[:, :], [(tga[:, kc, mm * 128:(mm + 1) * 128], hT[:, kc, b0:b1]) for kc in range(16)], [rga] + hr(b0, b1), r1)
                        pyp, r2 = ps_next()
                        mm_group(pyp[:, :], [(tpu[:, kc, mm * 128:(mm + 1) * 128], mixedT[:, kc, o0:o1]) for kc in range(8)], [rpu] + mres, r2)
                        pgb, r3 = ps_next()
                        mm_group(pgb[:, :], [(tgb[:, kc, mm * 128:(mm + 1) * 128], hT[:, kc, b0:b1]) for kc in range(16)], [rgb] + hr(b0, b1), r3)
                        pya, r4 = ps_next()
                        mm_group(pya[:, :], [(tau[:, kc, mm * 128:(mm + 1) * 128], qa[:, kc, o0:o1]) for kc in range(8)], [rau] + [qares[c_][n_] for c_ in range(8) for n_ in range(8)], r4)
                        S.op("act", lambda e, pga=pga: e.activation(out=sga_t[0], in_=pga[:, :], func=AF.Sigmoid), reads=[r1], writes=[sga_r[0]])
                        S.op("act", lambda e, pgb=pgb: e.activation(out=sga_t[1], in_=pgb[:, :], func=AF.Sigmoid), reads=[r3], writes=[sga_r[1]])
                        S.op("dve", lambda e, pyp=pyp: e.tensor_tensor(out=t_t[0], in0=sga_t[0], in1=pyp[:, :], op=ALU.mult),
                             reads=[sga_r[0], r2], writes=[t_r[0]])
                        S.op("dve", lambda e, pya=pya: e.tensor_tensor(out=t_t[1], in0=sga_t[1], in1=pya[:, :], op=ALU.mult),
                             reads=[sga_r[1], r4], writes=[t_r[1]])
                        S.op("dve", lambda e, sl_=sl_, mm=mm, o0=o0, o1=o1: e.tensor_tensor(out=merged[sl_][mm][:, o0:o1], in0=t_t[0], in1=t_t[1], op=ALU.add),
                             reads=[t_r[0], t_r[1]], writes=[mgres[sl_][mm]])
                W.retire()

            uc = [0]
            xo_t = [tmpC[:, 0:512], tmpC[:, 512:1024]]
            xo_r = [Res(), Res()]

            def stage_down(mp):
                s_, r_ = W.pop(("wo", mp), [(v3(2, 2048), w_o.rearrange("(j p) n -> p j n", p=128)[:, 2 * mp:2 * mp + 2, :])])
                two = v3(2, 2048)(s_)
                sl_ = mp % 2
                for m in range(NCH):
                    for (b0, b1) in TB2:
                        o0, o1 = b0 - HALO, b1 - HALO
                        po, ro = ps_next()
                        mm_group(po[:, :], [(two[:, jj, m * 128:(m + 1) * 128], merged[sl_][jj][:, o0:o1]) for jj in range(2)],
                                 [r_, mgres[sl_][0], mgres[sl_][1]], ro)
                        uc[0] += 1
                        if m % 3 == 2:
                            qq = uc[0] % 2
                            S.op("act", lambda e, po=po, m=m, qq=qq: e.activation(out=xo_t[qq], in_=po[:, :], func=AF.Copy,
                                                                                   scale=gate_v[1][:, m:m + 1]),
                                 reads=[ro, R["der"]], writes=[xo_r[qq]])
                            S.op("pool", lambda e, m=m, b0=b0, b1=b1, qq=qq: e.tensor_tensor(out=xT[:, m, b0:b1], in0=xT[:, m, b0:b1],
                                                                                              in1=xo_t[qq], op=ALU.add),
                                 reads=[xo_r[qq], xres[m]], writes=[xres[m]])
                        else:
                            S.op("dve", lambda e, po=po, m=m, b0=b0, b1=b1: e.scalar_tensor_tensor(
                                out=xT[:, m, b0:b1], in0=po[:, :], scalar=gate_v[1][:, m:m + 1], in1=xT[:, m, b0:b1], op0=ALU.mult, op1=ALU.add),
                                reads=[ro, xres[m], R["der"]], writes=[xres[m]])
                W.retire()

            for mp in range(9):
                if mp < 8:
                    stage_gu(mp)
                    compute_mod(8, [mp])
                if mp >= 1:
                    stage_down(mp - 1)

        S.plan = True
        program()
        S.plan = False
        S.reset()
        program()
        assert W.pos == len(W.tiles), (W.pos, len(W.tiles))

        engmap = {"pe": "tensor", "act": "scalar", "dve": "vector", "pool": "gpsimd", "sp": "sync"}
        with nc.Block() as block:
            def replay(en):
                def body(e):
                    for item in S.prog[en]:
                        if item[0] == "w":
                            e.wait_ge(sems[item[1]], item[2])
                        else:
                            ins = item[1](e)
                            ins.then_inc(sems[item[2]], item[3])
                return body
            block.tensor(replay("pe"))
            block.scalar(replay("act"))
            block.vector(replay("dve"))
            block.gpsimd(replay("pool"))
            block.sync(replay("sp"))
    return nc


def _rel_bucket_band():
    ql = np.arange(128)[:, None]
    j = np.arange(256)[None, :]
    n = np.clip(128 + ql - j, 0, None)
    nf = np.maximum(n, 1).astype(np.float32)
    large = 16 + (np.log(nf / 16) / np.log(128 / 16) * 16).astype(np.int32)
    large = np.minimum(large, 31)
    return np.where(n < 16, n, large).astype(np.int32)


def _host_layout(inputs):
    f = lambda a: np.ascontiguousarray(np.asarray(a, dtype=np.float32))
    x = f(inputs["x"])[0]
    xTfull = np.ascontiguousarray(x.T)
    par = np.zeros((128, 219), np.float32)
    par[:, 0:16] = f(inputs["c"])[0].reshape(16, 128).T
    par[:, 16:160] = f(inputs["b_ada"])[0].reshape(144, 128).T
    for i, nm in enumerate(("g_ffn1", "g_mix", "g_ffn2")):
        par[:, 160 + 16 * i:176 + 16 * i] = f(inputs[nm])[0].reshape(16, 128).T
    par[:, 208:216] = f(inputs["pool_scale"])[0].reshape(8, 128).T
    par[:, 216] = np.tile(f(inputs["q_gain"])[0], 2)
    par[:, 217] = np.tile(f(inputs["k_gain"])[0], 2)
    sinks = f(inputs["sinks"])[0]
    sink_tab = np.zeros((128, 2, 4, 128), np.float32)
    for kvh in range(2):
        for c_ in range(4):
            for p in range(2):
                sink_tab[p * 64:(p + 1) * 64, kvh, c_, :] = sinks[8 * kvh + 2 * c_ + p]
    rb = f(inputs["rel_bias"])
    bucket = _rel_bucket_band()
    ql = np.arange(128)[:, None]
    jb = np.arange(256)[None, :]
    dist = 128 + ql - jb
    ok = (dist >= 0) & (dist < 128)
    bias_tab = np.zeros((128, 2, 2, 2, 4, 128), np.float32)
    for kvh in range(2):
        for kb in range(2):
            for p in range(2):
                for c_ in range(4):
                    h = 8 * kvh + 2 * c_ + p
                    bq = rb[bucket[:, kb * 128:(kb + 1) * 128], h]
                    bq = np.where(ok[:, kb * 128:(kb + 1) * 128], bq, np.float32(MASKV))
                    bias_tab[:, kvh, kb, p, c_, :] = bq.T
    shared = {
        "par": par,
        "bias_tab": np.ascontiguousarray(bias_tab.reshape(128, 4096)),
        "sink_tab": np.ascontiguousarray(sink_tab.reshape(128, 1024)),
        "w_ada": f(inputs["w_ada"])[0],
        "w_ffn1_gu": f(inputs["w_ffn1_gu"])[0],
        "w_ffn1_down": f(inputs["w_ffn1_down"])[0],
        "w_in": f(inputs["w_in"])[0],
        "pool_mix": f(inputs["pool_mix"])[0],
        "w_pool_up": f(inputs["w_pool_up"])[0],
        "w_attn_up": f(inputs["w_attn_up"])[0],
        "w_o": f(inputs["w_o"])[0],
        "w_ffn2_gu": f(inputs["w_ffn2_gu"])[0],
        "w_ffn2_down": f(inputs["w_ffn2_down"])[0],
    }
    in_maps = []
    for core in range(NCORES):
        t0 = core * TO
        xt = np.zeros((D, TT), np.float32)
        if core > 0:
            xt[:, :] = xTfull[:, t0 - HALO:t0 + TO]
        else:
            xt[:, HALO:] = xTfull[:, 0:TO]
        cpar = np.zeros((128, 65), np.float32)
        cpar[:, 0] = 1.0 if core > 0 else 0.0
        for g, w in enumerate((2, 4, 8, 16)):
            for t in range(16):
                cnt = w if core > 0 else min(t + 1, w)
                cpar[:, 1 + g * 16 + t] = np.float32(1.0) / np.float32(cnt)
        m = dict(shared)
        m["xT"] = xt
        m["cpar"] = cpar
        in_maps.append(m)
    return in_maps


_NC_CACHE = {}


def kernel(**inputs):
    in_maps = _host_layout(inputs)
    if "nc" not in _NC_CACHE:
        _NC_CACHE["nc"] = build_program()
    nc = _NC_CACHE["nc"]
    res = run_bass_kernel_spmd(nc, in_maps, core_ids=list(range(NCORES)))
    outs = [np.asarray(r["yT"]) for r in res.results]
    y = np.concatenate([o.T for o in outs], axis=0)
    return np.ascontiguousarray(y[None].astype(np.float32))
```
